# Optimizing a Trainium2 kernel written in Bass

```python
import jax, jax.numpy as jnp
from jax import lax
import numpy as np

D_MODEL = 1024
BATCH = 8
SEQ = 4096
DEPTH = 2

CHUNK = 64
Q_BLOCK = 2 * CHUNK
N_META = 16
N_MIXERS = 2
POOL_WINDOWS = (2, 4, 8, 16)
N_POOL_GROUPS = len(POOL_WINDOWS)
POOL_GROUP = D_MODEL // N_POOL_GROUPS
MAX_WINDOW = max(POOL_WINDOWS)
N_HEADS = 16
HEAD_DIM = D_MODEL // N_HEADS
D_FF = -(-8 * D_MODEL // (3 * 256)) * 256
N_POOL_LAYERS = (DEPTH + 1) // 2
N_FOX_LAYERS = DEPTH // 2
DN_ALPHA = (2.0 * DEPTH) ** 0.25
DN_BETA = (8.0 * DEPTH) ** -0.25
LN_EPS = 1e-5

kernel_name = "hybrid_pool_fox_deepnorm_meta"


def layer_norm(x, g, b):
    xf = x.astype(jnp.float32)
    mu = jnp.mean(xf, axis=-1, keepdims=True)
    var = jnp.mean(jnp.square(xf - mu), axis=-1, keepdims=True)
    y = (xf - mu) * lax.rsqrt(var + LN_EPS)
    return (y * g.astype(jnp.float32) + b.astype(jnp.float32)).astype(x.dtype)


def pool_mixer(x, w_grp, scale):
    B, L, D = x.shape
    xf = x.astype(jnp.float32)
    P = jnp.pad(jnp.cumsum(xf, axis=1), ((0, 0), (MAX_WINDOW, 0), (0, 0)))
    t1 = jnp.arange(1, L + 1, dtype=jnp.float32)
    groups = []
    for g, w in enumerate(POOL_WINDOWS):
        sl = slice(g * POOL_GROUP, (g + 1) * POOL_GROUP)
        win_sum = P[:, MAX_WINDOW:MAX_WINDOW + L, sl] - P[:, MAX_WINDOW - w:MAX_WINDOW - w + L, sl]
        cnt = jnp.minimum(t1, float(w))[None, :, None]
        groups.append(win_sum / cnt - xf[:, :, sl])
    y = jnp.stack(groups, axis=2).astype(x.dtype)
    y = jnp.einsum('blgc,gce->blge', y, w_grp).reshape(B, L, D)
    return y * scale


def fox_mixer(x, w_in, b_f, w_o):
    B, L, D = x.shape
    proj = jnp.einsum('bld,de->ble', x, w_in)
    q, k, v, f_logit = jnp.split(proj, [D, 2 * D, 3 * D], axis=-1)
    to_heads = lambda t: t.reshape(B, L, N_HEADS, HEAD_DIM).transpose(0, 2, 1, 3)
    q, k, v = to_heads(q), to_heads(k), to_heads(v)
    log_f = jax.nn.log_sigmoid((f_logit + b_f).astype(jnp.float32))
    c = jnp.cumsum(log_f, axis=1).transpose(0, 2, 1)
    Lp = -(-L // Q_BLOCK) * Q_BLOCK
    pad = Lp - L
    padl = lambda t: jnp.pad(t, ((0, 0), (0, 0), (0, pad), (0, 0)))
    q, k, v = padl(q), padl(k), padl(v)
    c = jnp.pad(c, ((0, 0), (0, 0), (0, pad)))
    scale = HEAD_DIM ** -0.5
    neg = jnp.finfo(jnp.float32).min
    outs = []
    for qb in range(Lp // Q_BLOCK):
        q0, q1 = qb * Q_BLOCK, (qb + 1) * Q_BLOCK
        s = jnp.einsum('bhqd,bhkd->bhqk', q[:, :, q0:q1], k[:, :, :q1],
                       preferred_element_type=jnp.float32) * scale
        s = s + c[:, :, q0:q1, None] - c[:, :, None, :q1]
        mask = jnp.arange(q0, q1)[:, None] >= jnp.arange(q1)[None, :]
        s = jnp.where(mask[None, None], s, neg)
        p = jax.nn.softmax(s, axis=-1).astype(v.dtype)
        outs.append(jnp.einsum('bhqk,bhkd->bhqd', p, v[:, :, :q1]))
    o = jnp.concatenate(outs, axis=2)[:, :, :L]
    o = o.transpose(0, 2, 1, 3).reshape(B, L, D)
    return jnp.einsum('bld,de->ble', o, w_o)


def swiglu(x, w_gate, w_up, w_down):
    hg = jnp.einsum('bld,df->blf', x, w_gate)
    hu = jnp.einsum('bld,df->blf', x, w_up)
    return jnp.einsum('blf,fd->bld', jax.nn.silu(hg) * hu, w_down)


def setup_inputs(seed: int = 0) -> dict:
    key = jax.random.key(seed)
    ks = jax.random.split(key, 16)
    nrm = lambda k, shp: jax.random.normal(k, shp, dtype=jnp.float32)
    D, H, C, F = D_MODEL, N_HEADS, POOL_GROUP, D_FF
    x = nrm(ks[0], (BATCH, SEQ, D))
    meta_tokens = nrm(ks[1], (N_META, D))
    pool_w = nrm(ks[2], (N_POOL_LAYERS, N_POOL_GROUPS, C, C)) * (C ** -0.5) * DN_BETA
    pool_scale = 1.0 + 0.02 * nrm(ks[3], (N_POOL_LAYERS, D))
    w_qk = nrm(ks[4], (N_FOX_LAYERS, D, 2 * D)) * (D ** -0.5)
    w_v = nrm(ks[5], (N_FOX_LAYERS, D, D)) * (D ** -0.5) * DN_BETA
    w_f = nrm(ks[6], (N_FOX_LAYERS, D, H)) * (D ** -0.5)
    fox_w_in = jnp.concatenate([w_qk, w_v, w_f], axis=-1)
    fox_b_f = 2.0 + 0.1 * nrm(ks[7], (N_FOX_LAYERS, H))
    fox_w_o = nrm(ks[8], (N_FOX_LAYERS, D, D)) * (D ** -0.5) * DN_BETA
    ffn_w_gate = nrm(ks[9], (DEPTH, D, F)) * (D ** -0.5) * DN_BETA
    ffn_w_up = nrm(ks[10], (DEPTH, D, F)) * (D ** -0.5) * DN_BETA
    ffn_w_down = nrm(ks[11], (DEPTH, F, D)) * (F ** -0.5) * DN_BETA
    ln_g = 1.0 + 0.02 * nrm(ks[12], (DEPTH, 2, D))
    ln_b = 0.02 * nrm(ks[13], (DEPTH, 2, D))
    return {"x": x, "meta_tokens": meta_tokens, "pool_w": pool_w, "pool_scale": pool_scale,
            "fox_w_in": fox_w_in, "fox_b_f": fox_b_f, "fox_w_o": fox_w_o,
            "ffn_w_gate": ffn_w_gate, "ffn_w_up": ffn_w_up, "ffn_w_down": ffn_w_down,
            "ln_g": ln_g, "ln_b": ln_b}


def reference(x, meta_tokens, pool_w, pool_scale, fox_w_in, fox_b_f, fox_w_o,
              ffn_w_gate, ffn_w_up, ffn_w_down, ln_g, ln_b):
    B = x.shape[0]
    meta = jnp.broadcast_to(meta_tokens[None].astype(x.dtype), (B, N_META, D_MODEL))
    h = jnp.concatenate([meta, x], axis=1)
    for i in range(DEPTH):
        j = i // N_MIXERS
        if i % N_MIXERS == 0:
            m = pool_mixer(h, pool_w[j], pool_scale[j])
        else:
            m = fox_mixer(h, fox_w_in[j], fox_b_f[j], fox_w_o[j])
        h = layer_norm(DN_ALPHA * h + m, ln_g[i, 0], ln_b[i, 0])
        f = swiglu(h, ffn_w_gate[i], ffn_w_up[i], ffn_w_down[i])
        h = layer_norm(DN_ALPHA * h + f, ln_g[i, 1], ln_b[i, 1])
    return h[:, N_META:]
```

```python
import contextlib
import numpy as np
import concourse.bass as bass
import concourse.mybir as mybir
from concourse.bass_utils import run_bass_kernel_spmd

F32 = mybir.dt.float32
BF16 = mybir.dt.bfloat16
AF = mybir.ActivationFunctionType
ALU = mybir.AluOpType

D = 1024
SEQ = 4096
NMETA = 16
NH = 16
DFF = 2816
NFC = DFF // 128
TS = 512
NSUP = SEQ // TS
LTOT = SEQ + NMETA
NKT = 1 + SEQ // 128
ALPHA = float((2.0 * 2) ** 0.25)
EPS = 1e-5
POOL_WINDOWS = (2, 4, 8, 16)
NSLOT = 4
WAVE_FFN = True
WAVE_QKV = True
SLOT_ELEMS = 4096

ENGINES = ("pe", "act", "dve", "pool", "sp")


class Op:
    __slots__ = ("engine", "fn", "deps", "sig", "sig_index", "is_dma", "dsem", "dtarget")

    def __init__(self, engine, fn, is_dma):
        self.engine = engine
        self.fn = fn
        self.deps = []
        self.sig = False
        self.sig_index = 0
        self.is_dma = is_dma
        self.dsem = None
        self.dtarget = 0


class Sched:
    def __init__(self, nc, n_dma_sems=12):
        self.nc = nc
        self.ops = {e: [] for e in ENGINES}
        self.res = {}
        self.n_dma_sems = n_dma_sems
        self.dma_rr = 0
        self.dma_count = [0] * n_dma_sems
        self.dma_last = [None] * n_dma_sems

    def _track(self, op, reads, writes):
        deps = {}
        for r in reads:
            st = self.res.get(r)
            if st is not None and st[0] is not None:
                deps[id(st[0])] = st[0]
        for r in writes:
            st = self.res.get(r)
            if st is not None:
                if st[0] is not None:
                    deps[id(st[0])] = st[0]
                for o in st[1]:
                    deps[id(o)] = o
        for r in reads:
            st = self.res.setdefault(r, [None, []])
            st[1].append(op)
        for r in writes:
            self.res[r] = [op, []]
        deps.pop(id(op), None)
        for d in deps.values():
            if d.engine == "pe" and op.engine == "pe" and not d.is_dma and not op.is_dma:
                continue
            op.deps.append(d)

    def op(self, engine, fn, reads=(), writes=()):
        o = Op(engine, fn, False)
        self._track(o, reads, writes)
        self.ops[engine].append(o)
        return o

    def dma(self, queue, fn, reads=(), writes=()):
        o = Op(queue, fn, True)
        self._track(o, reads, writes)
        s = self.dma_rr
        self.dma_rr = (self.dma_rr + 1) % self.n_dma_sems
        prev = self.dma_last[s]
        if prev is not None:
            o.deps.append(prev)
        self.dma_count[s] += 16
        o.dsem = s
        o.dtarget = self.dma_count[s]
        self.dma_last[s] = o
        self.ops[queue].append(o)
        return o

    def build(self, st):
        nc = self.nc
        for e in ENGINES:
            for o in self.ops[e]:
                for d in o.deps:
                    d.sig = True
        for e in ENGINES:
            cnt = 0
            for o in self.ops[e]:
                if not o.is_dma and o.sig:
                    cnt += 1
                    o.sig_index = cnt
        esem = {e: st.enter_context(nc.semaphore("sem_" + e)) for e in ENGINES}
        dsem = [st.enter_context(nc.semaphore("dsem%d" % i)) for i in range(self.n_dma_sems)]
        block = st.enter_context(nc.Block())
        sched = self

        def emit(ename, eng):
            waited = {}
            for o in sched.ops[ename]:
                need = {}
                for d in o.deps:
                    if d.is_dma:
                        key = ("d", d.dsem)
                        val = d.dtarget
                    else:
                        key = ("e", d.engine)
                        val = d.sig_index
                    if val > need.get(key, 0):
                        need[key] = val
                for key, val in need.items():
                    if waited.get(key, 0) >= val:
                        continue
                    waited[key] = val
                    sem = dsem[key[1]] if key[0] == "d" else esem[key[1]]
                    eng.wait_ge(sem, val)
                ins = o.fn(eng)
                if o.is_dma:
                    ins.then_inc(dsem[o.dsem], 16)
                elif o.sig:
                    ins.then_inc(esem[ename], 1)
            for i in range(sched.n_dma_sems):
                if sched.dma_count[i] > waited.get(("d", i), 0):
                    eng.wait_ge(dsem[i], sched.dma_count[i])

        @block.sync
        def _(eng):
            emit("sp", eng)

        @block.tensor
        def _(eng):
            emit("pe", eng)

        @block.scalar
        def _(eng):
            emit("act", eng)

        @block.vector
        def _(eng):
            emit("dve", eng)

        @block.gpsimd
        def _(eng):
            emit("pool", eng)


def _bf16_round(a):
    a = np.asarray(a, np.float32)
    u = a.view(np.uint32).astype(np.uint64)
    u = (u + 0x7FFF + ((u >> 16) & 1)) & 0xFFFF0000
    return u.astype(np.uint32).view(np.float32)


def make_consts():
    c = {}
    c["identf"] = np.eye(128, dtype=np.float32)
    s = np.arange(128)[:, None]
    t = np.arange(128)[None, :]
    atd = np.zeros((128, 4, 128), np.float32)
    atoff = np.zeros((128, 4, 128), np.float32)
    atm = np.zeros((16, 4, 16), np.float32)
    for g, w in enumerate(POOL_WINDOWS):
        d = t - s
        atd[:, g, :] = np.where((d >= 0) & (d < w), 1.0 / w, 0.0) - (s == t)
        d2 = t - s + 128
        atoff[:, g, :] = np.where(d2 < w, 1.0 / w, 0.0)
        sm = np.arange(16)[:, None]
        tm = np.arange(16)[None, :]
        dm = tm - sm
        cnt = np.minimum(tm + 1, w).astype(np.float32)
        atm[:, g, :] = np.where((dm >= 0) & (dm < w), 1.0 / cnt, 0.0) - (sm == tm)
    c["atd"] = atd
    c["atoff"] = atoff
    c["atoffm"] = np.ascontiguousarray(atoff[112:128])
    hi = _bf16_round(atm)
    c["atmhi"] = hi
    c["atmlo"] = (atm - hi).astype(np.float32)
    selc = np.zeros((128, 4, 128), np.float32)
    selc[:, 0, :] = (s <= t)
    selc[127, 1, :] = 1.0
    selc[15, 2, :] = 1.0
    selc[0, 3, :] = 1.0
    c["selc"] = selc
    c["maskt"] = (s <= t).astype(np.float32)
    return c


CONST_SHAPES = {
    "identf": [128, 128], "atd": [128, 4, 128], "atoff": [128, 4, 128], "atoffm": [16, 4, 128],
    "atmhi": [16, 4, 16], "atmlo": [16, 4, 16], "selc": [128, 4, 128], "maskt": [128, 128],
}


def build_program(nsup=NSUP):
    nc = bass.Bass("TRN2", target_bir_lowering=False)
    din = lambda name, shape: nc.dram_tensor(name, shape, F32, kind="ExternalInput").ap()
    x_d = din("x", [SEQ, D])
    meta_d = din("meta", [NMETA, D])
    poolw_d = din("pool_w", [4, 256, 256])
    win_d = din("w_in", [D, 3 * D + NH])
    wo_d = din("w_o", [D, D])
    wg_d = din("w_gate", [2, D, DFF])
    wu_d = din("w_up", [2, D, DFF])
    wd_d = din("w_down", [2, DFF, D])
    vecs_d = din("vecs", [72, 128])
    bfbc_d = din("bf_bc", [128, NH])
    cst_d = {k: din(k, v) for k, v in CONST_SHAPES.items()}
    out_d = nc.dram_tensor("out", [SEQ, D], F32, kind="ExternalOutput").ap()
    kscr = nc.dram_tensor("kscr", [8, 128, LTOT], BF16).ap()
    vscr = nc.dram_tensor("vscr", [8, 128, NKT, 192], BF16).ap()
    wscr = nc.dram_tensor("wscr", [46, 128, SLOT_ELEMS], BF16).ap()

    S = Sched(nc)
    with contextlib.ExitStack() as st:
        sb = lambda name, shape, dt: st.enter_context(nc.sbuf_tensor(name, shape, dt))
        h = sb("h", [128, 8, TS], F32)
        hb = sb("hb", [128, 8, TS], BF16)
        A = sb("A", [128, 24, TS], BF16)
        xin = sb("xin", [128, 2, D], F32)
        xbr = sb("xbr", [128, 3, D], BF16)
        xpre = sb("xpre", [128, D], F32)
        zb = sb("zb", [128, 2, TS], BF16)
        zq = sb("zq", [128, 2, TS], BF16)
        msb = sb("msb", [128, TS], F32)
        m2 = sb("m2", [128, TS], F32)
        rstd = sb("rstd", [128, TS], F32)
        nmr = sb("nmr", [128, TS], F32)
        tmp = sb("tmp", [128, 2, TS], F32)
        sg = sb("sg", [128, 2, TS], F32)
        PT = sb("PT", [128, 3, TS], BF16)
        oev = sb("oev", [128, 2, TS], F32)
        KTs = sb("KTs", [128, 8, TS], BF16)
        Vs = sb("Vs", [128, 4, 8 * 192], BF16)
        kbuf = sb("kbuf", [128, 2, LTOT], BF16)
        vbuf = sb("vbuf", [128, 2, NKT * 192], BF16)
        cpos = sb("cpos", [128, NKT, NH], F32)
        biasT = sb("biasT", [128, NKT, NH], F32)
        cref = sb("cref", [128, NH], F32)
        xf = sb("xf", [128, 4, NH], F32)
        wring = sb("wring", [128, NSLOT, SLOT_ELEMS], BF16)
        pw = sb("pw", [128, 4, 2, 256], BF16)
        wf = sb("wf", [128, 8, NH], BF16)
        vecs_sb = sb("vecs_sb", [72, 128], F32)
        vecT = sb("vecT", [128, 72], F32)
        bfbc = sb("bfbc", [128, NH], F32)
        identf = sb("identf_sb", [128, 128], F32)
        atd = sb("atd_sb", [128, 4, 128], BF16)
        atoff = sb("atoff_sb", [128, 4, 128], BF16)
        atoffm = sb("atoffm_sb", [16, 4, 128], BF16)
        atmhi = sb("atmhi_sb", [16, 4, 16], BF16)
        atmlo = sb("atmlo_sb", [16, 4, 16], BF16)
        selc = sb("selc_sb", [128, 4, 128], F32)
        maskt = sb("maskt_sb", [128, 128], BF16)
        onesf = sb("onesf", [128, 64], F32)
        onesm = sb("onesm", [128, 128], BF16)
        psb = [st.enter_context(nc.psum_tensor("ps%d" % i, [128, 512], F32)) for i in range(8)]

        rot = {"b": 0}

        def bank():
            b = rot["b"]
            rot["b"] = (b + 1) % 8
            return b

        ring = {}

        def nxt(name, n):
            v = ring.get(name, 0)
            ring[name] = (v + 1) % n
            return v

        S.dma("sp", lambda e: e.dma_start(out=identf[:], in_=cst_d["identf"]), writes=["identf"])
        S.dma("sp", lambda e: e.dma_start(out=selc[:], in_=cst_d["selc"]), writes=["selc"])
        S.dma("sp", lambda e: e.dma_start(out=vecs_sb[:], in_=vecs_d), writes=["vecs_sb"])
        S.dma("sp", lambda e: e.dma_start(out=bfbc[:], in_=bfbc_d), writes=["bfbc"])
        for name, t_ in (("atd", atd), ("atoff", atoff), ("atoffm", atoffm), ("atmhi", atmhi),
                         ("atmlo", atmlo), ("maskt", maskt)):
            S.dma("pool", lambda e, t_=t_, name=name: e.dma_start(out=t_[:], in_=cst_d[name]), writes=[name])
        S.dma("pool", lambda e: e.dma_start(
            out=pw[:], in_=poolw_d.rearrange("g (cc p) e -> p g cc e", p=128)), writes=["pw"])
        S.dma("pool", lambda e: e.dma_start(
            out=wf[:], in_=win_d.rearrange("(j p) c -> p j c", p=128)[:, :, 3 * D:3 * D + NH]), writes=["wf"])
        S.op("dve", lambda e: e.memset(onesf[:], 1.0), writes=["onesf"])
        S.op("dve", lambda e: e.memset(onesm[:], 1.0 / 1024.0), writes=["onesm"])
        S.op("dve", lambda e: e.memset(Vs[:].rearrange("p t (hp c) -> p t hp c", hp=8)[:, :, :, 64:128], 1.0),
             writes=[("Vs", t) for t in range(4)])
        b0 = bank()
        S.op("pe", lambda e: e.transpose(psb[b0][:, 0:72], vecs_sb[:, :], identf[0:72, 0:72]),
             reads=["vecs_sb", "identf"], writes=[("ps", b0)])
        S.op("dve", lambda e: e.tensor_copy(vecT[:], psb[b0][:, 0:72]), reads=[("ps", b0)], writes=["vecT"])

        def gvec(layer, sub, j):
            k = (layer * 2 + sub) * 8 + j
            return vecT[:, k:k + 1]

        def bvec(layer, sub, j):
            k = 32 + (layer * 2 + sub) * 8 + j
            return vecT[:, k:k + 1]

        def psvec(j):
            return vecT[:, 64 + j:65 + j]

        units = []

        def add_unit(parts, key, n):
            units.append((parts, key, n))
            return len(units) - 1

        wstate = {"issued": 0}
        wkeys = {}

        def wneed(u, base=None):
            lim = min(len(units), (u if base is None else base) + NSLOT)
            while wstate["issued"] < lim:
                v = wstate["issued"]
                slot = v % NSLOT
                parts, key, n = units[v]
                if key in wkeys:
                    kid = wkeys[key]
                    S.dma("pool", lambda e, slot=slot, kid=kid, n=n: e.dma_start(out=wring[:, slot, 0:n],
                                                                               in_=wscr[kid][:, 0:n]),
                          reads=[("wscr", kid)], writes=[("w", slot, 0), ("w", slot, 1)])
                else:
                    kid = len(wkeys)
                    wkeys[key] = kid
                    for k, (dst_fn, src) in enumerate(parts):
                        dst = dst_fn(wring[:, slot, :])
                        S.dma("pool", lambda e, dst=dst, src=src: e.dma_start(out=dst, in_=src),
                              writes=[("w", slot, k)])
                    S.dma("sp", lambda e, slot=slot, kid=kid, n=n: e.dma_start(out=wscr[kid][:, 0:n],
                                                                             in_=wring[:, slot, 0:n]),
                          reads=[("w", slot, 0), ("w", slot, 1)], writes=[("wscr", kid)])
                wstate["issued"] += 1
            return u % NSLOT

        def wres(slot):
            return [("w", slot, 0), ("w", slot, 1)]

        def units_ffn(l):
            gu = []
            for ug in range(NFC // 2):
                c0 = ug * 2
                parts = []
                for k, wsrc in enumerate((wg_d, wu_d)):
                    src = wsrc[l].rearrange("(j p) c -> p j c", p=128)[:, :, c0 * 128:c0 * 128 + 256]
                    parts.append((lambda sl, k=k: sl.rearrange("p (a j c) -> p a j c", a=2, j=8)[:, k], src))
                gu.append(add_unit(parts, ("gu", l, ug), SLOT_ELEMS))
            dn = []
            for j in range(8):
                src = wd_d[l].rearrange("(c p) e -> p c e", p=128)[:, :, j * 128:(j + 1) * 128]
                dn.append(add_unit([(lambda sl: sl[:, 0:NFC * 128].rearrange("p (c e) -> p c e", c=NFC), src)],
                                   ("dn", l, j), NFC * 128))
            return gu, dn

        def units_qkv():
            qk = []
            for uq in range(4):
                parts = []
                for k in range(2):
                    src = win_d.rearrange("(j p) c -> p j c", p=128)[:, :, k * D + uq * 256:k * D + uq * 256 + 256]
                    parts.append((lambda sl, k=k: sl.rearrange("p (a j c) -> p a j c", a=2, j=8)[:, k], src))
                qk.append(add_unit(parts, ("qk", uq), SLOT_ELEMS))
            vv = []
            for hf in range(2):
                src = win_d.rearrange("(j p) c -> p j c", p=128)[:, :, 2 * D + hf * 512:2 * D + hf * 512 + 512]
                vv.append(add_unit([(lambda sl: sl.rearrange("p (j c) -> p j c", j=8), src)], ("v", hf), SLOT_ELEMS))
            return qk, vv

        def units_wo():
            res = []
            for uo in range(2):
                src = wo_d.rearrange("(hp p) e -> p hp e", p=128)[:, :, uo * 512:uo * 512 + 512]
                res.append(add_unit([(lambda sl: sl.rearrange("p (hp e) -> p hp e", hp=8), src)], ("wo", uo), SLOT_ELEMS))
            return res

        def layer_norm(T, layer, sub):
            bm = bank()
            bq = bank()
            for j in range(8):
                r = nxt("zb", 2)
                S.op("act", lambda e, j=j, r=r: e.activation(zb[:, r, 0:T], h[:, j, 0:T], AF.Identity),
                     reads=[("h", j)], writes=[("zb", r)])
                S.op("act", lambda e, j=j, r=r: e.activation(zq[:, r, 0:T], h[:, j, 0:T], AF.Square),
                     reads=[("h", j)], writes=[("zq", r)])
                S.op("pe", lambda e, j=j, r=r: e.matmul(psb[bm][:, 0:T], onesm[:, :], zb[:, r, 0:T],
                                                       start=(j == 0), stop=(j == 7)),
                     reads=[("zb", r), "onesm"], writes=[("ps", bm)])
                S.op("pe", lambda e, j=j, r=r: e.matmul(psb[bq][:, 0:T], onesm[:, :], zq[:, r, 0:T],
                                                       start=(j == 0), stop=(j == 7)),
                     reads=[("zq", r), "onesm"], writes=[("ps", bq)])
            S.op("act", lambda e: e.activation(msb[:, 0:T], psb[bm][:, 0:T], AF.Identity),
                 reads=[("ps", bm)], writes=["msb"])
            S.op("act", lambda e: e.activation(m2[:, 0:T], psb[bm][:, 0:T], AF.Square),
                 reads=[("ps", bm)], writes=["m2"])
            S.op("dve", lambda e: e.tensor_tensor(m2[:, 0:T], psb[bq][:, 0:T], m2[:, 0:T], ALU.subtract),
                 reads=[("ps", bq), "m2"], writes=["m2"])
            S.op("act", lambda e: e.activation(rstd[:, 0:T], m2[:, 0:T], AF.Ln, bias=EPS),
                 reads=["m2"], writes=["rstd"])
            S.op("act", lambda e: e.activation(rstd[:, 0:T], rstd[:, 0:T], AF.Exp, scale=-0.5),
                 reads=["rstd"], writes=["rstd"])
            S.op("dve", lambda e: e.scalar_tensor_tensor(nmr[:, 0:T], msb[:, 0:T], -1.0, rstd[:, 0:T],
                                                         ALU.mult, ALU.mult),
                 reads=["msb", "rstd"], writes=["nmr"])
            for j in range(8):
                r = nxt("tmp", 2)
                S.op("dve", lambda e, j=j, r=r: e.tensor_tensor(tmp[:, r, 0:T], h[:, j, 0:T], rstd[:, 0:T], ALU.mult),
                     reads=[("h", j), "rstd"], writes=[("tmp", r)])
                S.op("dve", lambda e, j=j, r=r: e.tensor_tensor(tmp[:, r, 0:T], tmp[:, r, 0:T], nmr[:, 0:T], ALU.add),
                     reads=[("tmp", r), "nmr"], writes=[("tmp", r)])
                S.op("act", lambda e, j=j, r=r: e.activation(h[:, j, 0:T], tmp[:, r, 0:T], AF.Identity,
                                                             bias=bvec(layer, sub, j), scale=gvec(layer, sub, j)),
                     reads=[("tmp", r), "vecT"], writes=[("h", j)])
                S.op("act", lambda e, j=j, r=r: e.activation(hb[:, j, 0:T], tmp[:, r, 0:T], AF.Identity,
                                                             bias=bvec(layer, sub, j), scale=gvec(layer, sub, j)),
                     reads=[("tmp", r), "vecT"], writes=[("hb", j)])

        def ffn(T, layer, gu, dn):
            if WAVE_FFN:
                s0 = wneed(gu[0])
                wv0 = wring[:, s0, :].rearrange("p (a j c) -> p a j c", a=2, j=8)
                wb = [[bank() for _ in range(2)] for _ in range(2)]
                for j in range(8):
                    def mmw(e, j=j):
                        ins = None
                        for c in range(2):
                            for k in range(2):
                                ins = e.matmul(psb[wb[c][k]][:, 0:T], wv0[:, k, j, c * 128:c * 128 + 128],
                                               hb[:, j, 0:T], start=(j == 0), stop=(j == 7))
                        return ins
                    S.op("pe", mmw, reads=wres(s0) + [("hb", j)],
                         writes=[("ps", wb[c][k]) for c in range(2) for k in range(2)])
                for c in range(2):
                    bg, bu = wb[c]
                    r = nxt("sg", 2)
                    S.op("act", lambda e, bg=bg, r=r: e.activation(sg[:, r, 0:T], psb[bg][:, 0:T], AF.Silu),
                         reads=[("ps", bg)], writes=[("sg", r)])
                    S.op("dve", lambda e, bu=bu, r=r, c=c: e.tensor_tensor(A[:, c, 0:T], sg[:, r, 0:T],
                                                                         psb[bu][:, 0:T], ALU.mult),
                         reads=[("sg", r), ("ps", bu)], writes=[("A", c)])
            for ug in range(1 if WAVE_FFN else 0, NFC // 2):
                slot = wneed(gu[ug])
                wv = wring[:, slot, :].rearrange("p (a j c) -> p a j c", a=2, j=8)
                for cc in range(2):
                    c = ug * 2 + cc
                    bg = bank()
                    bu = bank()

                    def mm(e, wv=wv, cc=cc, bg=bg, bu=bu):
                        ins = None
                        for k, bb in ((0, bg), (1, bu)):
                            for j in range(8):
                                ins = e.matmul(psb[bb][:, 0:T], wv[:, k, j, cc * 128:(cc + 1) * 128], hb[:, j, 0:T],
                                               start=(j == 0), stop=(j == 7))
                        return ins
                    S.op("pe", mm, reads=wres(slot) + [("hb", j) for j in range(8)],
                         writes=[("ps", bg), ("ps", bu)])
                    r = nxt("sg", 2)
                    S.op("act", lambda e, bg=bg, r=r: e.activation(sg[:, r, 0:T], psb[bg][:, 0:T], AF.Silu),
                         reads=[("ps", bg)], writes=[("sg", r)])
                    S.op("dve", lambda e, bu=bu, r=r, c=c: e.tensor_tensor(A[:, c, 0:T], sg[:, r, 0:T],
                                                                         psb[bu][:, 0:T], ALU.mult),
                         reads=[("sg", r), ("ps", bu)], writes=[("A", c)])
            for j in range(8):
                slot = wneed(dn[j])
                wv = wring[:, slot, 0:NFC * 128].rearrange("p (c e) -> p c e", c=NFC)
                bf = bank()

                def mm(e, wv=wv, bf=bf):
                    ins = None
                    for c in range(NFC):
                        ins = e.matmul(psb[bf][:, 0:T], wv[:, c, :], A[:, c, 0:T], start=(c == 0), stop=(c == NFC - 1))
                    return ins
                S.op("pe", mm, reads=wres(slot) + [("A", c) for c in range(NFC)], writes=[("ps", bf)])
                S.op("dve", lambda e, j=j, bf=bf: e.scalar_tensor_tensor(h[:, j, 0:T], h[:, j, 0:T], ALPHA,
                                                                         psb[bf][:, 0:T], ALU.mult, ALU.add),
                     reads=[("h", j), ("ps", bf)], writes=[("h", j)])
            layer_norm(T, layer, 1)

        prev_x = {"slot": None, "tp": 0}

        def pool_layer(T, src_rows, pre=False):
            nt = len(src_rows)
            for tt, (src, tp) in enumerate(src_rows):
                if pre and tt == 0:
                    xt_ = xpre[0:tp, :]
                    xkey = "xpre"
                else:
                    s = nxt("xin", 2)
                    S.dma("sp", lambda e, s=s, src=src, tp=tp: e.dma_start(out=xin[0:tp, s, :], in_=src),
                          writes=[("xin", s)])
                    xt_ = xin[0:tp, s, :]
                    xkey = ("xin", s)
                r = nxt("xbr", 3)
                S.op("act", lambda e, xt_=xt_, r=r, tp=tp: e.activation(xbr[0:tp, r, :], xt_, AF.Identity),
                     reads=[xkey], writes=[("xbr", r)])
                for jb in range(2):
                    b = bank()

                    def tr(e, xt_=xt_, tp=tp, jb=jb, b=b):
                        ins = None
                        for jj in range(4):
                            j = jb * 4 + jj
                            ins = e.transpose(psb[b][:, jj * 128:jj * 128 + tp], xt_[:, j * 128:(j + 1) * 128],
                                              identf[0:tp, 0:tp])
                        return ins
                    S.op("pe", tr, reads=[xkey, "identf"], writes=[("ps", b)])
                    S.op("act", lambda e, tp=tp, jb=jb, b=b, tt=tt: e.activation(
                        h[:, jb * 4:jb * 4 + 4, tt * 128:tt * 128 + tp],
                        psb[b][:, :].rearrange("p (a c) -> p a c", a=4)[:, :, 0:tp], AF.Identity, scale=ALPHA),
                        reads=[("ps", b)], writes=[("h", jb * 4 + jj) for jj in range(4)])
                meta_tile = (tp == NMETA)
                pslot, ptp = prev_x["slot"], prev_x["tp"]
                for jb in range(2):
                    b = bank()

                    def bm(e, r=r, tp=tp, jb=jb, b=b, meta_tile=meta_tile, pslot=pslot, ptp=ptp):
                        ins = None
                        for jj in range(4):
                            j = jb * 4 + jj
                            g = j // 2
                            o = psb[b][:, jj * 128:jj * 128 + tp]
                            xl = xbr[0:tp, r, j * 128:(j + 1) * 128]
                            if meta_tile:
                                e.matmul(o, xl, atmhi[0:tp, g, 0:tp], start=True, stop=False)
                                ins = e.matmul(o, xl, atmlo[0:tp, g, 0:tp], start=False, stop=True)
                            else:
                                e.matmul(o, xl, atd[:, g, :], start=True, stop=False)
                                xp = xbr[0:ptp, pslot, j * 128:(j + 1) * 128]
                                rhs = atoffm[0:16, g, :] if ptp == NMETA else atoff[:, g, :]
                                ins = e.matmul(o, xp, rhs, start=False, stop=True)
                        return ins
                    rd = [("xbr", r), "atd", "atoff", "atoffm", "atmhi", "atmlo"]
                    if pslot is not None:
                        rd.append(("xbr", pslot))
                    S.op("pe", bm, reads=rd, writes=[("ps", b)])
                    S.op("dve", lambda e, tp=tp, jb=jb, b=b, tt=tt: e.tensor_copy(
                        A[:, jb * 4:jb * 4 + 4, tt * 128:tt * 128 + tp],
                        psb[b][:, :].rearrange("p (a c) -> p a c", a=4)[:, :, 0:tp]),
                        reads=[("ps", b)], writes=[("A", jb * 4 + jj) for jj in range(4)])
                prev_x["slot"], prev_x["tp"] = r, tp
            for je in range(8):
                g = je // 2
                b = bank()

                def mm(e, je=je, g=g, b=b):
                    e.matmul(psb[b][:, 0:T], pw[:, g, 0, (je % 2) * 128:(je % 2) * 128 + 128], A[:, 2 * g, 0:T],
                             start=True, stop=False)
                    return e.matmul(psb[b][:, 0:T], pw[:, g, 1, (je % 2) * 128:(je % 2) * 128 + 128],
                                    A[:, 2 * g + 1, 0:T], start=False, stop=True)
                S.op("pe", mm, reads=["pw", ("A", 2 * g), ("A", 2 * g + 1)], writes=[("ps", b)])
                S.op("dve", lambda e, je=je, b=b: e.scalar_tensor_tensor(h[:, je, 0:T], psb[b][:, 0:T], psvec(je),
                                                                         h[:, je, 0:T], ALU.mult, ALU.add),
                     reads=[("ps", b), ("h", je), "vecT"], writes=[("h", je)])
            layer_norm(T, 0, 0)

        def qkv(T, tps, kt0, tok0, qk, vv, want_q):
            if want_q:
                for hd in range(NH):
                    zp = 64 if hd % 2 == 0 else 0
                    S.op("dve", lambda e, hd=hd, zp=zp: e.memset(A[zp:zp + 64, hd, :], 0.0),
                         writes=[("A", hd)])
            ks = (0, 1) if want_q else (1,)

            def qk_evac(hp, k, b):
                if k == 0:
                    S.op("act", lambda e: e.activation(A[0:64, 2 * hp, 0:T], psb[b][0:64, 0:T],
                                                       AF.Identity, scale=0.125),
                         reads=[("ps", b)], writes=[("A", 2 * hp)])
                    S.op("act", lambda e: e.activation(A[64:128, 2 * hp + 1, 0:T],
                                                       psb[b][64:128, 0:T], AF.Identity, scale=0.125),
                         reads=[("ps", b)], writes=[("A", 2 * hp + 1)])
                else:
                    S.op("dve", lambda e: e.tensor_copy(KTs[:, hp, 0:T], psb[b][:, 0:T]),
                         reads=[("ps", b)], writes=[("KTs", hp)])

            s0 = wneed(qk[0])
            s1 = wneed(qk[1], base=qk[0])
            wvs = [wring[:, sl, :].rearrange("p (a j c) -> p a j c", a=2, j=8) for sl in (s0, s1)]
            wb = {(hp, k): bank() for hp in range(4) for k in ks}
            for j in (range(8) if WAVE_QKV else ()):
                def mmw(e, j=j):
                    ins = None
                    for hp in range(4):
                        for k in ks:
                            ins = e.matmul(psb[wb[(hp, k)]][:, 0:T],
                                           wvs[hp // 2][:, k, j, (hp % 2) * 128:(hp % 2) * 128 + 128],
                                           hb[:, j, 0:T], start=(j == 0), stop=(j == 7))
                    return ins
                S.op("pe", mmw, reads=wres(s0) + wres(s1) + [("hb", j)], writes=[("ps", b_) for b_ in wb.values()])
            for hp in (range(4) if WAVE_QKV else ()):
                for k in ks:
                    qk_evac(hp, k, wb[(hp, k)])
            for tt, tp in enumerate(tps):
                kt = kt0 + tt
                b = bank()

                def mm(e, tt=tt, tp=tp, b=b):
                    ins = None
                    for j in range(8):
                        ins = e.matmul(psb[b][0:tp, 0:NH], hb[:, j, tt * 128:tt * 128 + tp], wf[:, j, :],
                                       start=(j == 0), stop=(j == 7))
                    return ins
                S.op("pe", mm, reads=["wf"] + [("hb", j) for j in range(8)], writes=[("ps", b)])
                r = tt
                S.op("dve", lambda e, tp=tp, b=b, r=r: e.tensor_tensor(xf[0:tp, r, :], psb[b][0:tp, 0:NH], bfbc[0:tp, :],
                                                                       ALU.add),
                     reads=[("ps", b), "bfbc"], writes=[("xf", r)])
                S.op("act", lambda e, tp=tp, r=r: e.activation(xf[0:tp, r, :], xf[0:tp, r, :], AF.Exp, scale=-1.0),
                     reads=[("xf", r)], writes=[("xf", r)])
                S.op("act", lambda e, tp=tp, r=r: e.activation(xf[0:tp, r, :], xf[0:tp, r, :], AF.Ln, bias=1.0),
                     reads=[("xf", r)], writes=[("xf", r)])

            def cs_step(tt):
                tp = tps[tt]
                kt = kt0 + tt
                r = tt
                b2 = bank()

                def cs(e):
                    first = (kt == 0)
                    ins = e.matmul(psb[b2][0:tp, 0:NH], selc[0:tp, 0, 0:tp], xf[0:tp, r, :], start=True, stop=first)
                    if not first:
                        if kt == 1:
                            ins = e.matmul(psb[b2][0:tp, 0:NH], selc[0:16, 2, 0:tp], cpos[0:16, 0, :],
                                           start=False, stop=True)
                        else:
                            ins = e.matmul(psb[b2][0:tp, 0:NH], selc[:, 1, 0:tp], cpos[:, kt - 1, :],
                                           start=False, stop=True)
                    return ins
                rd = [("xf", r), "selc"]
                if kt > 0:
                    rd.append(("cpos", kt - 1))
                S.op("pe", cs, reads=rd, writes=[("ps", b2)])
                S.op("dve", lambda e: e.tensor_copy(cpos[0:tp, kt, :], psb[b2][0:tp, 0:NH]),
                     reads=[("ps", b2)], writes=[("cpos", kt)])
            for uq in range(2 if WAVE_QKV else 0, 4):
                slot = wneed(qk[uq])
                wv = wring[:, slot, :].rearrange("p (a j c) -> p a j c", a=2, j=8)
                for hpp in range(2):
                    hp = uq * 2 + hpp
                    for k in ks:
                        b = bank()

                        def mm(e, wv=wv, k=k, hpp=hpp, b=b):
                            ins = None
                            for j in range(8):
                                ins = e.matmul(psb[b][:, 0:T], wv[:, k, j, hpp * 128:(hpp + 1) * 128], hb[:, j, 0:T],
                                               start=(j == 0), stop=(j == 7))
                            return ins
                        S.op("pe", mm, reads=wres(slot) + [("hb", j) for j in range(8)], writes=[("ps", b)])
                        qk_evac(hp, k, b)
            S.dma("sp", lambda e: e.dma_start(
                out=kscr.rearrange("hp p t -> p hp t")[:, :, tok0:tok0 + T], in_=KTs[:, :, 0:T]),
                reads=[("KTs", hp) for hp in range(8)], writes=[("kscr", tok0)])
            for hf in range(2):
                slot = wneed(vv[hf])
                wv = wring[:, slot, :].rearrange("p (j c) -> p j c", j=8)
                for tt, tp in enumerate(tps):
                    b = bank()

                    def mm(e, wv=wv, tt=tt, tp=tp, b=b):
                        ins = None
                        for j in range(8):
                            ins = e.matmul(psb[b][0:tp, :], hb[:, j, tt * 128:tt * 128 + tp], wv[:, j, :],
                                           start=(j == 0), stop=(j == 7))
                        return ins
                    S.op("pe", mm, reads=wres(slot) + [("hb", j) for j in range(8)], writes=[("ps", b)])
                    S.op("act", lambda e, tt=tt, tp=tp, b=b, hf=hf: e.activation(
                        Vs[0:tp, tt, :].rearrange("p (hp a c) -> p hp a c", hp=8, a=3)[:, hf * 4:hf * 4 + 4, 0:3:2, :],
                        psb[b][0:tp, :].rearrange("p (hp a c) -> p hp a c", hp=4, a=2), AF.Identity),
                        reads=[("ps", b), ("Vs", tt)], writes=[("Vs", tt, hf)])
                    if hf == 0:
                        cs_step(tt)
            for tt, tp in enumerate(tps):
                kt = kt0 + tt
                S.dma("sp", lambda e, tt=tt, tp=tp, kt=kt: e.dma_start(
                    out=vscr.rearrange("hp p k c -> p hp k c")[0:tp, :, kt, :],
                    in_=Vs[0:tp, tt, :].rearrange("p (hp c) -> p hp c", hp=8)),
                    reads=[("Vs", tt, 0), ("Vs", tt, 1), ("Vs", tt)], writes=[("vscr", kt)])

        def attention(i, wo_units):
            T = TS
            nkt = 1 + 4 * (i + 1)
            nkeys = NMETA + TS * (i + 1)
            ktref = 4 * i + 1 + 2
            b = bank()
            S.op("pe", lambda e: e.matmul(psb[b][:, 0:NH], selc[:, 3, :], cpos[:, ktref, :], start=True, stop=True),
                 reads=["selc", ("cpos", ktref)], writes=[("ps", b)])
            S.op("dve", lambda e: e.tensor_copy(cref[:, :], psb[b][:, 0:NH]), reads=[("ps", b)], writes=["cref"])
            S.op("dve", lambda e: e.tensor_tensor(biasT[:, 0:nkt, :], cpos[:, 0:nkt, :],
                                                  cref[:, :].unsqueeze(1).broadcast_to([128, nkt, NH]), ALU.subtract),
                 reads=["cref"] + [("cpos", k) for k in range(nkt)], writes=["biasT"])
            kv_res = [("kscr", t) for t in [0] + [NMETA + TS * q for q in range(i + 1)]]
            v_res = [("vscr", k) for k in range(nkt)]

            def load_kv(hp):
                s = hp % 2
                S.dma("sp", lambda e, hp=hp, s=s: e.dma_start(out=kbuf[:, s, 0:nkeys], in_=kscr[hp][:, 0:nkeys]),
                      reads=kv_res, writes=[("kbuf", s)])
                S.dma("sp", lambda e, hp=hp, s=s: e.dma_start(
                    out=vbuf[:, s, 0:nkt * 192], in_=vscr[hp][:, 0:nkt, :].rearrange("p k c -> p (k c)")),
                    reads=v_res, writes=[("vbuf", s)])
            LOOK = 3
            tiles = []
            for hd in range(NH):
                for kt in range(nkt):
                    tiles.append((hd, kt))
            sb_of = {}
            pending = []

            def issue_s(idx):
                hd, kt = tiles[idx]
                hp, hh = hd // 2, hd % 2
                s = hp % 2
                p0 = 64 * hh
                nk = NMETA if kt == 0 else 128
                kc0 = 0 if kt == 0 else NMETA + 128 * (kt - 1)
                dg = kt - (4 * i + 1)
                q0 = 128 * dg if dg > 0 else 0
                bs = nxt("sbank", 5)
                sb_of[idx] = bs
                S.op("pe", lambda e: e.matmul(
                    psb[bs][0:nk, q0:T], kbuf[:, s, kc0:kc0 + nk], A[:, hd, q0:T],
                    start=True, stop=True),
                    reads=[("kbuf", s), ("A", hd)], writes=[("ps", bs)])

            load_kv(0)
            for idx in range(min(LOOK, len(tiles))):
                issue_s(idx)
            for idx, (hd, kt) in enumerate(tiles):
                hp, hh = hd // 2, hd % 2
                s = hp % 2
                if kt == 0 and hh == 0 and hp + 1 < 8:
                    load_kv(hp + 1)
                bo = 6 + (hd % 2)
                nk = NMETA if kt == 0 else 128
                dg = kt - (4 * i + 1)
                q0 = 128 * dg if dg > 0 else 0
                bs = sb_of.pop(idx)
                r = nxt("PT", 3)
                S.op("act", lambda e, nk=nk, q0=q0, bs=bs, r=r, kt=kt, hd=hd: e.activation(
                    PT[0:nk, r, q0:T], psb[bs][0:nk, q0:T], AF.Exp, bias=biasT[0:nk, kt, hd:hd + 1]),
                    reads=[("ps", bs), "biasT"], writes=[("PT", r)])
                if dg >= 0:
                    S.op("dve", lambda e, q0=q0, r=r: e.tensor_tensor(
                        PT[:, r, q0:q0 + 128], PT[:, r, q0:q0 + 128], maskt[:, :], ALU.mult),
                        reads=[("PT", r), "maskt"], writes=[("PT", r)])
                if idx + LOOK < len(tiles):
                    issue_s(idx + LOOK)
                S.op("pe", lambda e, s=s, nk=nk, q0=q0, r=r, kt=kt, hh=hh, bo=bo: e.matmul(
                    psb[bo][:, q0:T], vbuf[0:nk, s, kt * 192 + hh * 64:kt * 192 + hh * 64 + 128],
                    PT[0:nk, r, q0:T], start=(kt == 0), stop=(kt == nkt - 1)),
                    reads=[("vbuf", s), ("PT", r)], writes=[("ps", bo)])
                if kt == nkt - 1:
                    r2 = nxt("oev", 2)
                    po = 64 * hh
                    pd = 64 - po
                    S.op("dve", lambda e, bo=bo, r2=r2, pd=pd: e.reciprocal(oev[pd:pd + 64, r2, :], psb[bo][pd:pd + 64, :]),
                         reads=[("ps", bo)], writes=[("oev", r2)])
                    S.op("dve", lambda e, bo=bo, r2=r2, po=po, pd=pd, hp=hp: e.tensor_tensor(
                        A[po:po + 64, 16 + hp, :], psb[bo][po:po + 64, :], oev[pd:pd + 64, r2, :], ALU.mult),
                        reads=[("oev", r2), ("ps", bo)], writes=[("A", 16 + hp)])
            for uo in range(2):
                slot = wneed(wo_units[uo])
                wv = wring[:, slot, :].rearrange("p (hp e) -> p hp e", hp=8)
                for jj in range(4):
                    j = uo * 4 + jj
                    bf = nxt("sbank", 5)

                    def mm(e, wv=wv, jj=jj, bf=bf):
                        ins = None
                        for hp in range(8):
                            ins = e.matmul(psb[bf][:, 0:T], wv[:, hp, jj * 128:(jj + 1) * 128], A[:, 16 + hp, 0:T],
                                           start=(hp == 0), stop=(hp == 7))
                        return ins
                    S.op("pe", mm, reads=wres(slot) + [("A", 16 + hp) for hp in range(8)], writes=[("ps", bf)])
                    S.op("dve", lambda e, j=j, bf=bf: e.scalar_tensor_tensor(h[:, j, 0:T], h[:, j, 0:T], ALPHA,
                                                                             psb[bf][:, 0:T], ALU.mult, ALU.add),
                         reads=[("h", j), ("ps", bf)], writes=[("h", j)])
            layer_norm(T, 1, 0)

        def write_out(i):
            def evac_store(tt, s, banks):
                S.op("act", lambda e: e.activation(xin[:, s, 0:512], psb[banks[0]][:, :], AF.Identity),
                     reads=[("ps", banks[0])], writes=[("xin", s)])
                S.op("dve", lambda e: e.tensor_copy(xin[:, s, 512:1024], psb[banks[1]][:, :]),
                     reads=[("ps", banks[1])], writes=[("xin", s, 1)])
                r0 = i * TS + tt * 128
                S.dma("sp", lambda e: e.dma_start(out=out_d[r0:r0 + 128, :], in_=xin[:, s, :]),
                      reads=[("xin", s), ("xin", s, 1)], writes=[("xin", s), ("xin", s, 1)])

            wb = {(tt, jb): bank() for tt in range(2) for jb in range(2)}
            for j in range(8):
                def trw(e, j=j):
                    ins = None
                    for tt in range(2):
                        ins = e.transpose(psb[wb[(tt, j // 4)]][:, (j % 4) * 128:(j % 4 + 1) * 128],
                                          h[:, j, tt * 128:(tt + 1) * 128], identf[:, :])
                    return ins
                S.op("pe", trw, reads=[("h", j), "identf"], writes=[("ps", wb[(tt, j // 4)]) for tt in range(2)])
            for tt in range(2):
                evac_store(tt, nxt("xin", 2), (wb[(tt, 0)], wb[(tt, 1)]))
            for tt in range(2, 4):
                s = nxt("xin", 2)
                banks = []
                for jb in range(2):
                    b = bank()
                    banks.append(b)

                    def tr(e, tt=tt, jb=jb, b=b):
                        ins = None
                        for jj in range(4):
                            ins = e.transpose(psb[b][:, jj * 128:(jj + 1) * 128],
                                              h[:, jb * 4 + jj, tt * 128:(tt + 1) * 128], identf[:, :])
                        return ins
                    S.op("pe", tr, reads=[("h", jb * 4 + jj) for jj in range(4)] + ["identf"], writes=[("ps", b)])
                evac_store(tt, s, banks)

        plan = []
        gu, dn = units_ffn(0)
        qk, vv = units_qkv()
        plan.append(("meta", gu, dn, qk, vv))
        for i in range(nsup):
            gu, dn = units_ffn(0)
            qk, vv = units_qkv()
            wo_u = units_wo()
            gu1, dn1 = units_ffn(1)
            plan.append((i, gu, dn, qk, vv, wo_u, gu1, dn1))

        _, gu, dn, qk, vv = plan[0]
        pool_layer(NMETA, [(meta_d, NMETA)])
        ffn(NMETA, 0, gu, dn)
        qkv(NMETA, [NMETA], 0, 0, qk, vv, want_q=False)
        for i in range(nsup):
            _, gu, dn, qk, vv, wo_u, gu1, dn1 = plan[1 + i]
            rows = [(x_d[i * TS + tt * 128:i * TS + (tt + 1) * 128, :], 128) for tt in range(4)]
            pool_layer(TS, rows, pre=(i > 0))
            ffn(TS, 0, gu, dn)
            qkv(TS, [128] * 4, 1 + 4 * i, NMETA + TS * i, qk, vv, want_q=True)
            attention(i, wo_u)
            if i + 1 < nsup:
                nsrc = x_d[(i + 1) * TS:(i + 1) * TS + 128, :]
                S.dma("sp", lambda e, nsrc=nsrc: e.dma_start(out=xpre[:, :], in_=nsrc), writes=["xpre"])
            ffn(TS, 1, gu1, dn1)
            write_out(i)
        S.build(st)
    return nc


_CACHE = {}


def kernel(x, meta_tokens, pool_w, pool_scale, fox_w_in, fox_b_f, fox_w_o,
           ffn_w_gate, ffn_w_up, ffn_w_down, ln_g, ln_b):
    f = lambda a: np.ascontiguousarray(np.asarray(a, dtype=np.float32))
    x = f(x)
    B = x.shape[0]
    if "nc" not in _CACHE:
        _CACHE["nc"] = build_program()
    nc = _CACHE["nc"]
    vecs = np.concatenate([f(ln_g).reshape(32, 128), f(ln_b).reshape(32, 128), f(pool_scale).reshape(8, 128)], axis=0)
    consts = make_consts()
    shared = {
        "meta": f(meta_tokens), "pool_w": f(pool_w)[0], "w_in": f(fox_w_in)[0], "w_o": f(fox_w_o)[0],
        "w_gate": f(ffn_w_gate), "w_up": f(ffn_w_up), "w_down": f(ffn_w_down),
        "vecs": np.ascontiguousarray(vecs),
        "bf_bc": np.ascontiguousarray(np.broadcast_to(f(fox_b_f).reshape(1, NH), (128, NH))),
    }
    shared.update(consts)
    in_maps = [dict(shared, x=x[b]) for b in range(B)]
    res = run_bass_kernel_spmd(nc, in_maps, core_ids=list(range(B)))
    return np.stack([np.asarray(r["out"], dtype=np.float32) for r in res.results], axis=0)
```

```python
import contextlib
import numpy as np
import concourse.bass as bass
import concourse.mybir as mybir
from concourse.bass_utils import run_bass_kernel_spmd

F32 = mybir.dt.float32
BF16 = mybir.dt.bfloat16
AF = mybir.ActivationFunctionType
ALU = mybir.AluOpType

D = 1024
SEQ = 4096
NMETA = 16
NH = 16
DFF = 2816
NFC = DFF // 128
TS = 512
NSUP = SEQ // TS
LTOT = SEQ + NMETA
NKT = 1 + SEQ // 128
ALPHA = float((2.0 * 2) ** 0.25)
EPS = 1e-5
POOL_WINDOWS = (2, 4, 8, 16)
NSLOT = 4
WAVE_FFN = True
WAVE_QKV = True
SLOT_ELEMS = 4096

ENGINES = ("pe", "act", "dve", "pool", "sp")


class Op:
    __slots__ = ("engine", "fn", "deps", "sig", "sig_index", "is_dma", "dsem", "dtarget")

    def __init__(self, engine, fn, is_dma):
        self.engine = engine
        self.fn = fn
        self.deps = []
        self.sig = False
        self.sig_index = 0
        self.is_dma = is_dma
        self.dsem = None
        self.dtarget = 0


class Sched:
    def __init__(self, nc, n_dma_sems=12):
        self.nc = nc
        self.ops = {e: [] for e in ENGINES}
        self.res = {}
        self.n_dma_sems = n_dma_sems
        self.dma_rr = 0
        self.dma_count = [0] * n_dma_sems
        self.dma_last = [None] * n_dma_sems
        self.n_sw = 0

    def _track(self, op, reads, writes):
        deps = {}
        for r in reads:
            st = self.res.get(r)
            if st is not None and st[0] is not None:
                deps[id(st[0])] = st[0]
        for r in writes:
            st = self.res.get(r)
            if st is not None:
                if st[0] is not None:
                    deps[id(st[0])] = st[0]
                for o in st[1]:
                    deps[id(o)] = o
        for r in reads:
            st = self.res.setdefault(r, [None, []])
            st[1].append(op)
        for r in writes:
            self.res[r] = [op, []]
        deps.pop(id(op), None)
        for d in deps.values():
            if d.engine == "pe" and op.engine == "pe" and not d.is_dma and not op.is_dma:
                continue
            op.deps.append(d)

    def op(self, engine, fn, reads=(), writes=()):
        o = Op(engine, fn, False)
        self._track(o, reads, writes)
        self.ops[engine].append(o)
        return o

    def dma(self, queue, fn, reads=(), writes=()):
        o = Op(queue, fn, True)
        self._track(o, reads, writes)
        if queue == "pool":
            o.dsem = ("sw", self.n_sw)
            o.dtarget = 16
            self.n_sw += 1
            self.ops[queue].append(o)
            return o
        s = self.dma_rr
        self.dma_rr = (self.dma_rr + 1) % self.n_dma_sems
        prev = self.dma_last[s]
        if prev is not None:
            o.deps.append(prev)
        self.dma_count[s] += 16
        o.dsem = s
        o.dtarget = self.dma_count[s]
        self.dma_last[s] = o
        self.ops[queue].append(o)
        return o

    def build(self, st):
        nc = self.nc
        for e in ENGINES:
            for o in self.ops[e]:
                for d in o.deps:
                    d.sig = True
        for e in ENGINES:
            cnt = 0
            for o in self.ops[e]:
                if not o.is_dma and o.sig:
                    cnt += 1
                    o.sig_index = cnt
        esem = {e: st.enter_context(nc.semaphore("sem_" + e)) for e in ENGINES}
        dsem = {i: st.enter_context(nc.semaphore("dsem%d" % i)) for i in range(self.n_dma_sems)}
        for i in range(self.n_sw):
            dsem[("sw", i)] = st.enter_context(nc.semaphore("swsem%d" % i))
        block = st.enter_context(nc.Block())
        sched = self

        def emit(ename, eng):
            waited = {}
            for o in sched.ops[ename]:
                need = {}
                for d in o.deps:
                    if d.is_dma:
                        key = ("d", d.dsem)
                        val = d.dtarget
                    else:
                        key = ("e", d.engine)
                        val = d.sig_index
                    if val > need.get(key, 0):
                        need[key] = val
                for key, val in need.items():
                    if waited.get(key, 0) >= val:
                        continue
                    waited[key] = val
                    sem = dsem[key[1]] if key[0] == "d" else esem[key[1]]
                    eng.wait_ge(sem, val)
                ins = o.fn(eng)
                if o.is_dma:
                    ins.then_inc(dsem[o.dsem], 16)
                elif o.sig:
                    ins.then_inc(esem[ename], 1)
            if ename == "sp":
                for i in range(sched.n_dma_sems):
                    if sched.dma_count[i] > waited.get(("d", i), 0):
                        eng.wait_ge(dsem[i], sched.dma_count[i])
                for i in range(sched.n_sw):
                    if waited.get(("d", ("sw", i)), 0) < 16:
                        eng.wait_ge(dsem[("sw", i)], 16)

        @block.sync
        def _(eng):
            emit("sp", eng)

        @block.tensor
        def _(eng):
            emit("pe", eng)

        @block.scalar
        def _(eng):
            emit("act", eng)

        @block.vector
        def _(eng):
            emit("dve", eng)

        @block.gpsimd
        def _(eng):
            emit("pool", eng)


def _bf16_round(a):
    a = np.asarray(a, np.float32)
    u = a.view(np.uint32).astype(np.uint64)
    u = (u + 0x7FFF + ((u >> 16) & 1)) & 0xFFFF0000
    return u.astype(np.uint32).view(np.float32)


def make_consts():
    c = {}
    c["identf"] = np.eye(128, dtype=np.float32)
    s = np.arange(128)[:, None]
    t = np.arange(128)[None, :]
    atd = np.zeros((128, 4, 128), np.float32)
    atoff = np.zeros((128, 4, 128), np.float32)
    atm = np.zeros((16, 4, 16), np.float32)
    for g, w in enumerate(POOL_WINDOWS):
        d = t - s
        atd[:, g, :] = np.where((d >= 0) & (d < w), 1.0 / w, 0.0) - (s == t)
        d2 = t - s + 128
        atoff[:, g, :] = np.where(d2 < w, 1.0 / w, 0.0)
        sm = np.arange(16)[:, None]
        tm = np.arange(16)[None, :]
        dm = tm - sm
        cnt = np.minimum(tm + 1, w).astype(np.float32)
        atm[:, g, :] = np.where((dm >= 0) & (dm < w), 1.0 / cnt, 0.0) - (sm == tm)
    c["atd"] = atd
    c["atoff"] = atoff
    c["atoffm"] = np.ascontiguousarray(atoff[112:128])
    hi = _bf16_round(atm)
    c["atmhi"] = hi
    c["atmlo"] = (atm - hi).astype(np.float32)
    selc = np.zeros((128, 4, 128), np.float32)
    selc[:, 0, :] = (s <= t)
    selc[127, 1, :] = 1.0
    selc[15, 2, :] = 1.0
    selc[0, 3, :] = 1.0
    c["selc"] = selc
    c["maskt"] = (s <= t).astype(np.float32)
    return c


CONST_SHAPES = {
    "identf": [128, 128], "atd": [128, 4, 128], "atoff": [128, 4, 128], "atoffm": [16, 4, 128],
    "atmhi": [16, 4, 16], "atmlo": [16, 4, 16], "selc": [128, 4, 128], "maskt": [128, 128],
}


def build_program(nsup=NSUP):
    nc = bass.Bass("TRN2", target_bir_lowering=False)
    din = lambda name, shape: nc.dram_tensor(name, shape, F32, kind="ExternalInput").ap()
    x_d = din("x", [SEQ, D])
    meta_d = din("meta", [NMETA, D])
    poolw_d = din("pool_w", [4, 256, 256])
    win_d = din("w_in", [D, 3 * D + NH])
    wo_d = din("w_o", [D, D])
    wg_d = din("w_gate", [2, D, DFF])
    wu_d = din("w_up", [2, D, DFF])
    wd_d = din("w_down", [2, DFF, D])
    vecs_d = din("vecs", [72, 128])
    bfbc_d = din("bf_bc", [128, NH])
    cst_d = {k: din(k, v) for k, v in CONST_SHAPES.items()}
    out_d = nc.dram_tensor("out", [SEQ, D], F32, kind="ExternalOutput").ap()
    kscr = nc.dram_tensor("kscr", [8, 128, LTOT], BF16).ap()
    vscr = nc.dram_tensor("vscr", [8, 128, NKT, 192], BF16).ap()
    wscr = nc.dram_tensor("wscr", [46, 128, SLOT_ELEMS], BF16).ap()

    S = Sched(nc)
    with contextlib.ExitStack() as st:
        sb = lambda name, shape, dt: st.enter_context(nc.sbuf_tensor(name, shape, dt))
        h = sb("h", [128, 8, TS], F32)
        hb = sb("hb", [128, 8, TS], BF16)
        A = sb("A", [128, 24, TS], BF16)
        xin = sb("xin", [128, 2, D], F32)
        xbr = sb("xbr", [128, 3, D], BF16)
        xpre = sb("xpre", [128, D], F32)
        zb = sb("zb", [128, 2, TS], BF16)
        zq = sb("zq", [128, 2, TS], BF16)
        msb = sb("msb", [128, TS], F32)
        m2 = sb("m2", [128, TS], F32)
        rstd = sb("rstd", [128, TS], F32)
        nmr = sb("nmr", [128, TS], F32)
        tmp = sb("tmp", [128, 2, TS], F32)
        sg = sb("sg", [128, 2, TS], F32)
        PT = sb("PT", [128, 3, TS], BF16)
        oev = sb("oev", [128, 2, TS], F32)
        KTs = sb("KTs", [128, 8, TS], BF16)
        Vs = sb("Vs", [128, 4, 8 * 192], BF16)
        kbuf = sb("kbuf", [128, 2, LTOT], BF16)
        vbuf = sb("vbuf", [128, 2, NKT * 192], BF16)
        cpos = sb("cpos", [128, NKT, NH], F32)
        biasT = sb("biasT", [128, NKT, NH], F32)
        cref = sb("cref", [128, NH], F32)
        xf = sb("xf", [128, 2, NH], F32)
        wring = sb("wring", [128, NSLOT, SLOT_ELEMS], BF16)
        pw = sb("pw", [128, 4, 2, 256], BF16)
        wf = sb("wf", [128, 8, NH], BF16)
        vecs_sb = sb("vecs_sb", [72, 128], F32)
        vecT = sb("vecT", [128, 72], F32)
        bfbc = sb("bfbc", [128, NH], F32)
        identf = sb("identf_sb", [128, 128], F32)
        atd = sb("atd_sb", [128, 4, 128], BF16)
        atoff = sb("atoff_sb", [128, 4, 128], BF16)
        atoffm = sb("atoffm_sb", [16, 4, 128], BF16)
        atmhi = sb("atmhi_sb", [16, 4, 16], BF16)
        atmlo = sb("atmlo_sb", [16, 4, 16], BF16)
        selc = sb("selc_sb", [128, 4, 128], F32)
        maskt = sb("maskt_sb", [128, 128], BF16)
        onesf = sb("onesf", [128, 64], F32)
        onesm = sb("onesm", [128, 128], BF16)
        psb = [st.enter_context(nc.psum_tensor("ps%d" % i, [128, 512], F32)) for i in range(8)]

        rot = {"b": 0}

        def bank():
            b = rot["b"]
            rot["b"] = (b + 1) % 8
            return b

        ring = {}

        def nxt(name, n):
            v = ring.get(name, 0)
            ring[name] = (v + 1) % n
            return v

        S.dma("sp", lambda e: e.dma_start(out=identf[:], in_=cst_d["identf"]), writes=["identf"])
        S.dma("sp", lambda e: e.dma_start(out=selc[:], in_=cst_d["selc"]), writes=["selc"])
        S.dma("sp", lambda e: e.dma_start(out=vecs_sb[:], in_=vecs_d), writes=["vecs_sb"])
        S.dma("sp", lambda e: e.dma_start(out=bfbc[:], in_=bfbc_d), writes=["bfbc"])
        for name, t_ in (("atd", atd), ("atoff", atoff), ("atoffm", atoffm), ("atmhi", atmhi),
                         ("atmlo", atmlo), ("maskt", maskt)):
            S.dma("pool", lambda e, t_=t_, name=name: e.dma_start(out=t_[:], in_=cst_d[name]), writes=[name])
        S.dma("pool", lambda e: e.dma_start(
            out=pw[:], in_=poolw_d.rearrange("g (cc p) e -> p g cc e", p=128)), writes=["pw"])
        S.dma("pool", lambda e: e.dma_start(
            out=wf[:], in_=win_d.rearrange("(j p) c -> p j c", p=128)[:, :, 3 * D:3 * D + NH]), writes=["wf"])
        S.op("dve", lambda e: e.memset(onesf[:], 1.0), writes=["onesf"])
        S.op("dve", lambda e: e.memset(onesm[:], 1.0 / 1024.0), writes=["onesm"])
        S.op("dve", lambda e: e.memset(Vs[:].rearrange("p t (hp c) -> p t hp c", hp=8)[:, :, :, 64:128], 1.0),
             writes=[("Vs", t) for t in range(4)])
        b0 = bank()
        S.op("pe", lambda e: e.transpose(psb[b0][:, 0:72], vecs_sb[:, :], identf[0:72, 0:72]),
             reads=["vecs_sb", "identf"], writes=[("ps", b0)])
        S.op("dve", lambda e: e.tensor_copy(vecT[:], psb[b0][:, 0:72]), reads=[("ps", b0)], writes=["vecT"])

        def gvec(layer, sub, j):
            k = (layer * 2 + sub) * 8 + j
            return vecT[:, k:k + 1]

        def bvec(layer, sub, j):
            k = 32 + (layer * 2 + sub) * 8 + j
            return vecT[:, k:k + 1]

        def psvec(j):
            return vecT[:, 64 + j:65 + j]

        units = []

        def add_unit(parts, key, n):
            units.append((parts, key, n))
            return len(units) - 1

        wstate = {"issued": 0}
        wkeys = {}

        def wneed(u, base=None):
            lim = min(len(units), (u if base is None else base) + NSLOT)
            while wstate["issued"] < lim:
                v = wstate["issued"]
                slot = v % NSLOT
                parts, key, n = units[v]
                if key in wkeys:
                    kid = wkeys[key]
                    S.dma("sp", lambda e, slot=slot, kid=kid, n=n: e.dma_start(out=wring[:, slot, 0:n],
                                                                             in_=wscr[kid][:, 0:n]),
                          reads=[("wscr", kid)], writes=[("w", slot, 0), ("w", slot, 1)])
                else:
                    kid = len(wkeys)
                    wkeys[key] = kid
                    for k, (dst_fn, src) in enumerate(parts):
                        dst = dst_fn(wring[:, slot, :])
                        S.dma("pool", lambda e, dst=dst, src=src: e.dma_start(out=dst, in_=src),
                              writes=[("w", slot, k)])
                    S.dma("sp", lambda e, slot=slot, kid=kid, n=n: e.dma_start(out=wscr[kid][:, 0:n],
                                                                             in_=wring[:, slot, 0:n]),
                          reads=[("w", slot, 0), ("w", slot, 1)], writes=[("wscr", kid)])
                wstate["issued"] += 1
            return u % NSLOT

        def wres(slot):
            return [("w", slot, 0), ("w", slot, 1)]

        def units_ffn(l):
            gu = []
            for ug in range(NFC // 2):
                c0 = ug * 2
                parts = []
                for k, wsrc in enumerate((wg_d, wu_d)):
                    src = wsrc[l].rearrange("(j p) c -> p j c", p=128)[:, :, c0 * 128:c0 * 128 + 256]
                    parts.append((lambda sl, k=k: sl.rearrange("p (a j c) -> p a j c", a=2, j=8)[:, k], src))
                gu.append(add_unit(parts, ("gu", l, ug), SLOT_ELEMS))
            dn = []
            for j in range(8):
                src = wd_d[l].rearrange("(c p) e -> p c e", p=128)[:, :, j * 128:(j + 1) * 128]
                dn.append(add_unit([(lambda sl: sl[:, 0:NFC * 128].rearrange("p (c e) -> p c e", c=NFC), src)],
                                   ("dn", l, j), NFC * 128))
            return gu, dn

        def units_qkv():
            qk = []
            for uq in range(4):
                parts = []
                for k in range(2):
                    src = win_d.rearrange("(j p) c -> p j c", p=128)[:, :, k * D + uq * 256:k * D + uq * 256 + 256]
                    parts.append((lambda sl, k=k: sl.rearrange("p (a j c) -> p a j c", a=2, j=8)[:, k], src))
                qk.append(add_unit(parts, ("qk", uq), SLOT_ELEMS))
            vv = []
            for hf in range(2):
                src = win_d.rearrange("(j p) c -> p j c", p=128)[:, :, 2 * D + hf * 512:2 * D + hf * 512 + 512]
                vv.append(add_unit([(lambda sl: sl.rearrange("p (j c) -> p j c", j=8), src)], ("v", hf), SLOT_ELEMS))
            return qk, vv

        def units_wo():
            res = []
            for uo in range(2):
                src = wo_d.rearrange("(hp p) e -> p hp e", p=128)[:, :, uo * 512:uo * 512 + 512]
                res.append(add_unit([(lambda sl: sl.rearrange("p (hp e) -> p hp e", hp=8), src)], ("wo", uo), SLOT_ELEMS))
            return res

        def layer_norm(T, layer, sub):
            bm = bank()
            bq = bank()
            for j in range(8):
                r = nxt("zb", 2)
                S.op("act", lambda e, j=j, r=r: e.activation(zb[:, r, 0:T], h[:, j, 0:T], AF.Identity),
                     reads=[("h", j)], writes=[("zb", r)])
                S.op("act", lambda e, j=j, r=r: e.activation(zq[:, r, 0:T], h[:, j, 0:T], AF.Square),
                     reads=[("h", j)], writes=[("zq", r)])
                S.op("pe", lambda e, j=j, r=r: e.matmul(psb[bm][:, 0:T], onesm[:, :], zb[:, r, 0:T],
                                                       start=(j == 0), stop=(j == 7)),
                     reads=[("zb", r), "onesm"], writes=[("ps", bm)])
                S.op("pe", lambda e, j=j, r=r: e.matmul(psb[bq][:, 0:T], onesm[:, :], zq[:, r, 0:T],
                                                       start=(j == 0), stop=(j == 7)),
                     reads=[("zq", r), "onesm"], writes=[("ps", bq)])
            S.op("act", lambda e: e.activation(msb[:, 0:T], psb[bm][:, 0:T], AF.Identity),
                 reads=[("ps", bm)], writes=["msb"])
            S.op("act", lambda e: e.activation(m2[:, 0:T], psb[bm][:, 0:T], AF.Square),
                 reads=[("ps", bm)], writes=["m2"])
            S.op("dve", lambda e: e.tensor_tensor(m2[:, 0:T], psb[bq][:, 0:T], m2[:, 0:T], ALU.subtract),
                 reads=[("ps", bq), "m2"], writes=["m2"])
            S.op("act", lambda e: e.activation(rstd[:, 0:T], m2[:, 0:T], AF.Ln, bias=EPS),
                 reads=["m2"], writes=["rstd"])
            S.op("act", lambda e: e.activation(rstd[:, 0:T], rstd[:, 0:T], AF.Exp, scale=-0.5),
                 reads=["rstd"], writes=["rstd"])
            S.op("dve", lambda e: e.scalar_tensor_tensor(nmr[:, 0:T], msb[:, 0:T], -1.0, rstd[:, 0:T],
                                                         ALU.mult, ALU.mult),
                 reads=["msb", "rstd"], writes=["nmr"])
            for j in range(8):
                r = nxt("tmp", 2)
                S.op("dve", lambda e, j=j, r=r: e.tensor_tensor(tmp[:, r, 0:T], h[:, j, 0:T], rstd[:, 0:T], ALU.mult),
                     reads=[("h", j), "rstd"], writes=[("tmp", r)])
                S.op("dve", lambda e, j=j, r=r: e.tensor_tensor(tmp[:, r, 0:T], tmp[:, r, 0:T], nmr[:, 0:T], ALU.add),
                     reads=[("tmp", r), "nmr"], writes=[("tmp", r)])
                S.op("act", lambda e, j=j, r=r: e.activation(h[:, j, 0:T], tmp[:, r, 0:T], AF.Identity,
                                                             bias=bvec(layer, sub, j), scale=gvec(layer, sub, j)),
                     reads=[("tmp", r), "vecT"], writes=[("h", j)])
                S.op("act", lambda e, j=j, r=r: e.activation(hb[:, j, 0:T], tmp[:, r, 0:T], AF.Identity,
                                                             bias=bvec(layer, sub, j), scale=gvec(layer, sub, j)),
                     reads=[("tmp", r), "vecT"], writes=[("hb", j)])

        def ffn(T, layer, gu, dn):
            if WAVE_FFN:
                s0 = wneed(gu[0])
                wv0 = wring[:, s0, :].rearrange("p (a j c) -> p a j c", a=2, j=8)
                wb = [[bank() for _ in range(2)] for _ in range(2)]
                for j in range(8):
                    def mmw(e, j=j):
                        ins = None
                        for c in range(2):
                            for k in range(2):
                                ins = e.matmul(psb[wb[c][k]][:, 0:T], wv0[:, k, j, c * 128:c * 128 + 128],
                                               hb[:, j, 0:T], start=(j == 0), stop=(j == 7))
                        return ins
                    S.op("pe", mmw, reads=wres(s0) + [("hb", j)],
                         writes=[("ps", wb[c][k]) for c in range(2) for k in range(2)])
                for c in range(2):
                    bg, bu = wb[c]
                    r = nxt("sg", 2)
                    S.op("act", lambda e, bg=bg, r=r: e.activation(sg[:, r, 0:T], psb[bg][:, 0:T], AF.Silu),
                         reads=[("ps", bg)], writes=[("sg", r)])
                    S.op("dve", lambda e, bu=bu, r=r, c=c: e.tensor_tensor(A[:, c, 0:T], sg[:, r, 0:T],
                                                                         psb[bu][:, 0:T], ALU.mult),
                         reads=[("sg", r), ("ps", bu)], writes=[("A", c)])
            for ug in range(1 if WAVE_FFN else 0, NFC // 2):
                slot = wneed(gu[ug])
                wv = wring[:, slot, :].rearrange("p (a j c) -> p a j c", a=2, j=8)
                for cc in range(2):
                    c = ug * 2 + cc
                    bg = bank()
                    bu = bank()

                    def mm(e, wv=wv, cc=cc, bg=bg, bu=bu):
                        ins = None
                        for k, bb in ((0, bg), (1, bu)):
                            for j in range(8):
                                ins = e.matmul(psb[bb][:, 0:T], wv[:, k, j, cc * 128:(cc + 1) * 128], hb[:, j, 0:T],
                                               start=(j == 0), stop=(j == 7))
                        return ins
                    S.op("pe", mm, reads=wres(slot) + [("hb", j) for j in range(8)],
                         writes=[("ps", bg), ("ps", bu)])
                    r = nxt("sg", 2)
                    S.op("act", lambda e, bg=bg, r=r: e.activation(sg[:, r, 0:T], psb[bg][:, 0:T], AF.Silu),
                         reads=[("ps", bg)], writes=[("sg", r)])
                    S.op("dve", lambda e, bu=bu, r=r, c=c: e.tensor_tensor(A[:, c, 0:T], sg[:, r, 0:T],
                                                                         psb[bu][:, 0:T], ALU.mult),
                         reads=[("sg", r), ("ps", bu)], writes=[("A", c)])
            for j in range(8):
                slot = wneed(dn[j])
                wv = wring[:, slot, 0:NFC * 128].rearrange("p (c e) -> p c e", c=NFC)
                bf = bank()

                def mm(e, wv=wv, bf=bf):
                    ins = None
                    for c in range(NFC):
                        ins = e.matmul(psb[bf][:, 0:T], wv[:, c, :], A[:, c, 0:T], start=(c == 0), stop=(c == NFC - 1))
                    return ins
                S.op("pe", mm, reads=wres(slot) + [("A", c) for c in range(NFC)], writes=[("ps", bf)])
                S.op("dve", lambda e, j=j, bf=bf: e.scalar_tensor_tensor(h[:, j, 0:T], h[:, j, 0:T], ALPHA,
                                                                         psb[bf][:, 0:T], ALU.mult, ALU.add),
                     reads=[("h", j), ("ps", bf)], writes=[("h", j)])
            layer_norm(T, layer, 1)

        prev_x = {"slot": None, "tp": 0}

        def pool_layer(T, src_rows, pre=False):
            nt = len(src_rows)
            for tt, (src, tp) in enumerate(src_rows):
                if pre and tt == 0:
                    xt_ = xpre[0:tp, :]
                    xkey = "xpre"
                else:
                    s = nxt("xin", 2)
                    S.dma("sp", lambda e, s=s, src=src, tp=tp: e.dma_start(out=xin[0:tp, s, :], in_=src),
                          writes=[("xin", s)])
                    xt_ = xin[0:tp, s, :]
                    xkey = ("xin", s)
                r = nxt("xbr", 3)
                S.op("act", lambda e, xt_=xt_, r=r, tp=tp: e.activation(xbr[0:tp, r, :], xt_, AF.Identity),
                     reads=[xkey], writes=[("xbr", r)])
                for jb in range(2):
                    b = bank()

                    def tr(e, xt_=xt_, tp=tp, jb=jb, b=b):
                        ins = None
                        for jj in range(4):
                            j = jb * 4 + jj
                            ins = e.transpose(psb[b][:, jj * 128:jj * 128 + tp], xt_[:, j * 128:(j + 1) * 128],
                                              identf[0:tp, 0:tp])
                        return ins
                    S.op("pe", tr, reads=[xkey, "identf"], writes=[("ps", b)])
                    S.op("act", lambda e, tp=tp, jb=jb, b=b, tt=tt: e.activation(
                        h[:, jb * 4:jb * 4 + 4, tt * 128:tt * 128 + tp],
                        psb[b][:, :].rearrange("p (a c) -> p a c", a=4)[:, :, 0:tp], AF.Identity, scale=ALPHA),
                        reads=[("ps", b)], writes=[("h", jb * 4 + jj) for jj in range(4)])
                meta_tile = (tp == NMETA)
                pslot, ptp = prev_x["slot"], prev_x["tp"]
                for jb in range(2):
                    b = bank()

                    def bm(e, r=r, tp=tp, jb=jb, b=b, meta_tile=meta_tile, pslot=pslot, ptp=ptp):
                        ins = None
                        for jj in range(4):
                            j = jb * 4 + jj
                            g = j // 2
                            o = psb[b][:, jj * 128:jj * 128 + tp]
                            xl = xbr[0:tp, r, j * 128:(j + 1) * 128]
                            if meta_tile:
                                e.matmul(o, xl, atmhi[0:tp, g, 0:tp], start=True, stop=False)
                                ins = e.matmul(o, xl, atmlo[0:tp, g, 0:tp], start=False, stop=True)
                            else:
                                e.matmul(o, xl, atd[:, g, :], start=True, stop=False)
                                xp = xbr[0:ptp, pslot, j * 128:(j + 1) * 128]
                                rhs = atoffm[0:16, g, :] if ptp == NMETA else atoff[:, g, :]
                                ins = e.matmul(o, xp, rhs, start=False, stop=True)
                        return ins
                    rd = [("xbr", r), "atd", "atoff", "atoffm", "atmhi", "atmlo"]
                    if pslot is not None:
                        rd.append(("xbr", pslot))
                    S.op("pe", bm, reads=rd, writes=[("ps", b)])
                    S.op("dve", lambda e, tp=tp, jb=jb, b=b, tt=tt: e.tensor_copy(
                        A[:, jb * 4:jb * 4 + 4, tt * 128:tt * 128 + tp],
                        psb[b][:, :].rearrange("p (a c) -> p a c", a=4)[:, :, 0:tp]),
                        reads=[("ps", b)], writes=[("A", jb * 4 + jj) for jj in range(4)])
                prev_x["slot"], prev_x["tp"] = r, tp
            for je in range(8):
                g = je // 2
                b = bank()

                def mm(e, je=je, g=g, b=b):
                    e.matmul(psb[b][:, 0:T], pw[:, g, 0, (je % 2) * 128:(je % 2) * 128 + 128], A[:, 2 * g, 0:T],
                             start=True, stop=False)
                    return e.matmul(psb[b][:, 0:T], pw[:, g, 1, (je % 2) * 128:(je % 2) * 128 + 128],
                                    A[:, 2 * g + 1, 0:T], start=False, stop=True)
                S.op("pe", mm, reads=["pw", ("A", 2 * g), ("A", 2 * g + 1)], writes=[("ps", b)])
                S.op("dve", lambda e, je=je, b=b: e.scalar_tensor_tensor(h[:, je, 0:T], psb[b][:, 0:T], psvec(je),
                                                                         h[:, je, 0:T], ALU.mult, ALU.add),
                     reads=[("ps", b), ("h", je), "vecT"], writes=[("h", je)])
            layer_norm(T, 0, 0)

        def qkv(T, tps, kt0, tok0, qk, vv, want_q):
            if want_q:
                for hd in range(NH):
                    zp = 64 if hd % 2 == 0 else 0
                    S.op("dve", lambda e, hd=hd, zp=zp: e.memset(A[zp:zp + 64, hd, :], 0.0),
                         writes=[("A", hd)])
            ks = (0, 1) if want_q else (1,)

            def qk_evac(hp, k, b):
                if k == 0:
                    S.op("act", lambda e: e.activation(A[0:64, 2 * hp, 0:T], psb[b][0:64, 0:T],
                                                       AF.Identity, scale=0.125),
                         reads=[("ps", b)], writes=[("A", 2 * hp)])
                    S.op("act", lambda e: e.activation(A[64:128, 2 * hp + 1, 0:T],
                                                       psb[b][64:128, 0:T], AF.Identity, scale=0.125),
                         reads=[("ps", b)], writes=[("A", 2 * hp + 1)])
                else:
                    S.op("dve", lambda e: e.tensor_copy(KTs[:, hp, 0:T], psb[b][:, 0:T]),
                         reads=[("ps", b)], writes=[("KTs", hp)])

            s0 = wneed(qk[0])
            s1 = wneed(qk[1], base=qk[0])
            wvs = [wring[:, sl, :].rearrange("p (a j c) -> p a j c", a=2, j=8) for sl in (s0, s1)]
            wb = {(hp, k): bank() for hp in range(4) for k in ks}
            for j in (range(8) if WAVE_QKV else ()):
                def mmw(e, j=j):
                    ins = None
                    for hp in range(4):
                        for k in ks:
                            ins = e.matmul(psb[wb[(hp, k)]][:, 0:T],
                                           wvs[hp // 2][:, k, j, (hp % 2) * 128:(hp % 2) * 128 + 128],
                                           hb[:, j, 0:T], start=(j == 0), stop=(j == 7))
                    return ins
                S.op("pe", mmw, reads=wres(s0) + wres(s1) + [("hb", j)], writes=[("ps", b_) for b_ in wb.values()])
            for hp in (range(4) if WAVE_QKV else ()):
                for k in ks:
                    qk_evac(hp, k, wb[(hp, k)])
            for uq in range(2 if WAVE_QKV else 0, 4):
                slot = wneed(qk[uq])
                wv = wring[:, slot, :].rearrange("p (a j c) -> p a j c", a=2, j=8)
                for hpp in range(2):
                    hp = uq * 2 + hpp
                    for k in ks:
                        b = bank()

                        def mm(e, wv=wv, k=k, hpp=hpp, b=b):
                            ins = None
                            for j in range(8):
                                ins = e.matmul(psb[b][:, 0:T], wv[:, k, j, hpp * 128:(hpp + 1) * 128], hb[:, j, 0:T],
                                               start=(j == 0), stop=(j == 7))
                            return ins
                        S.op("pe", mm, reads=wres(slot) + [("hb", j) for j in range(8)], writes=[("ps", b)])
                        qk_evac(hp, k, b)
            for hf in range(2):
                slot = wneed(vv[hf])
                wv = wring[:, slot, :].rearrange("p (j c) -> p j c", j=8)
                for tt, tp in enumerate(tps):
                    b = bank()

                    def mm(e, wv=wv, tt=tt, tp=tp, b=b):
                        ins = None
                        for j in range(8):
                            ins = e.matmul(psb[b][0:tp, :], hb[:, j, tt * 128:tt * 128 + tp], wv[:, j, :],
                                           start=(j == 0), stop=(j == 7))
                        return ins
                    S.op("pe", mm, reads=wres(slot) + [("hb", j) for j in range(8)], writes=[("ps", b)])
                    S.op("act", lambda e, tt=tt, tp=tp, b=b, hf=hf: e.activation(
                        Vs[0:tp, tt, :].rearrange("p (hp a c) -> p hp a c", hp=8, a=3)[:, hf * 4:hf * 4 + 4, 0:3:2, :],
                        psb[b][0:tp, :].rearrange("p (hp a c) -> p hp a c", hp=4, a=2), AF.Identity),
                        reads=[("ps", b), ("Vs", tt)], writes=[("Vs", tt, hf)])
            for tt, tp in enumerate(tps):
                kt = kt0 + tt
                b = bank()

                def mm(e, tt=tt, tp=tp, b=b):
                    ins = None
                    for j in range(8):
                        ins = e.matmul(psb[b][0:tp, 0:NH], hb[:, j, tt * 128:tt * 128 + tp], wf[:, j, :],
                                       start=(j == 0), stop=(j == 7))
                    return ins
                S.op("pe", mm, reads=["wf"] + [("hb", j) for j in range(8)], writes=[("ps", b)])
                r = nxt("xf", 2)
                S.op("dve", lambda e, tp=tp, b=b, r=r: e.tensor_tensor(xf[0:tp, r, :], psb[b][0:tp, 0:NH], bfbc[0:tp, :],
                                                                       ALU.add),
                     reads=[("ps", b), "bfbc"], writes=[("xf", r)])
                S.op("act", lambda e, tp=tp, r=r: e.activation(xf[0:tp, r, :], xf[0:tp, r, :], AF.Exp, scale=-1.0),
                     reads=[("xf", r)], writes=[("xf", r)])
                S.op("act", lambda e, tp=tp, r=r: e.activation(xf[0:tp, r, :], xf[0:tp, r, :], AF.Ln, bias=1.0),
                     reads=[("xf", r)], writes=[("xf", r)])
                b2 = bank()

                def cs(e, tp=tp, r=r, b2=b2, kt=kt):
                    first = (kt == 0)
                    ins = e.matmul(psb[b2][0:tp, 0:NH], selc[0:tp, 0, 0:tp], xf[0:tp, r, :], start=True, stop=first)
                    if not first:
                        if kt == 1:
                            ins = e.matmul(psb[b2][0:tp, 0:NH], selc[0:16, 2, 0:tp], cpos[0:16, 0, :],
                                           start=False, stop=True)
                        else:
                            ins = e.matmul(psb[b2][0:tp, 0:NH], selc[:, 1, 0:tp], cpos[:, kt - 1, :],
                                           start=False, stop=True)
                    return ins
                rd = [("xf", r), "selc"]
                if kt > 0:
                    rd.append(("cpos", kt - 1))
                S.op("pe", cs, reads=rd, writes=[("ps", b2)])
                S.op("dve", lambda e, tp=tp, b2=b2, kt=kt: e.tensor_copy(cpos[0:tp, kt, :], psb[b2][0:tp, 0:NH]),
                     reads=[("ps", b2)], writes=[("cpos", kt)])
            S.dma("sp", lambda e: e.dma_start(
                out=kscr.rearrange("hp p t -> p hp t")[:, :, tok0:tok0 + T], in_=KTs[:, :, 0:T]),
                reads=[("KTs", hp) for hp in range(8)], writes=[("kscr", tok0)])
            for tt, tp in enumerate(tps):
                kt = kt0 + tt
                S.dma("sp", lambda e, tt=tt, tp=tp, kt=kt: e.dma_start(
                    out=vscr.rearrange("hp p k c -> p hp k c")[0:tp, :, kt, :],
                    in_=Vs[0:tp, tt, :].rearrange("p (hp c) -> p hp c", hp=8)),
                    reads=[("Vs", tt, 0), ("Vs", tt, 1), ("Vs", tt)], writes=[("vscr", kt)])

        def attention(i, wo_units):
            T = TS
            nkt = 1 + 4 * (i + 1)
            nkeys = NMETA + TS * (i + 1)
            ktref = 4 * i + 1 + 2
            b = bank()
            S.op("pe", lambda e: e.matmul(psb[b][:, 0:NH], selc[:, 3, :], cpos[:, ktref, :], start=True, stop=True),
                 reads=["selc", ("cpos", ktref)], writes=[("ps", b)])
            S.op("dve", lambda e: e.tensor_copy(cref[:, :], psb[b][:, 0:NH]), reads=[("ps", b)], writes=["cref"])
            S.op("dve", lambda e: e.tensor_tensor(biasT[:, 0:nkt, :], cpos[:, 0:nkt, :],
                                                  cref[:, :].unsqueeze(1).broadcast_to([128, nkt, NH]), ALU.subtract),
                 reads=["cref"] + [("cpos", k) for k in range(nkt)], writes=["biasT"])
            kv_res = [("kscr", t) for t in [0] + [NMETA + TS * q for q in range(i + 1)]]
            v_res = [("vscr", k) for k in range(nkt)]

            def load_kv(hp):
                s = hp % 2
                S.dma("sp", lambda e, hp=hp, s=s: e.dma_start(out=kbuf[:, s, 0:nkeys], in_=kscr[hp][:, 0:nkeys]),
                      reads=kv_res, writes=[("kbuf", s)])
                S.dma("sp", lambda e, hp=hp, s=s: e.dma_start(
                    out=vbuf[:, s, 0:nkt * 192], in_=vscr[hp][:, 0:nkt, :].rearrange("p k c -> p (k c)")),
                    reads=v_res, writes=[("vbuf", s)])
            LOOK = 3
            tiles = []
            for hd in range(NH):
                for kt in range(nkt):
                    tiles.append((hd, kt))
            sb_of = {}
            pending = []

            def issue_s(idx):
                hd, kt = tiles[idx]
                hp, hh = hd // 2, hd % 2
                s = hp % 2
                p0 = 64 * hh
                nk = NMETA if kt == 0 else 128
                kc0 = 0 if kt == 0 else NMETA + 128 * (kt - 1)
                dg = kt - (4 * i + 1)
                q0 = 128 * dg if dg > 0 else 0
                bs = nxt("sbank", 5)
                sb_of[idx] = bs
                S.op("pe", lambda e: e.matmul(
                    psb[bs][0:nk, q0:T], kbuf[:, s, kc0:kc0 + nk], A[:, hd, q0:T],
                    start=True, stop=True),
                    reads=[("kbuf", s), ("A", hd)], writes=[("ps", bs)])

            load_kv(0)
            for idx in range(min(LOOK, len(tiles))):
                issue_s(idx)
            for idx, (hd, kt) in enumerate(tiles):
                hp, hh = hd // 2, hd % 2
                s = hp % 2
                if kt == 0 and hh == 0 and hp + 1 < 8:
                    load_kv(hp + 1)
                bo = 6 + (hd % 2)
                nk = NMETA if kt == 0 else 128
                dg = kt - (4 * i + 1)
                q0 = 128 * dg if dg > 0 else 0
                bs = sb_of.pop(idx)
                r = nxt("PT", 3)
                S.op("act", lambda e, nk=nk, q0=q0, bs=bs, r=r, kt=kt, hd=hd: e.activation(
                    PT[0:nk, r, q0:T], psb[bs][0:nk, q0:T], AF.Exp, bias=biasT[0:nk, kt, hd:hd + 1]),
                    reads=[("ps", bs), "biasT"], writes=[("PT", r)])
                if dg >= 0:
                    S.op("dve", lambda e, q0=q0, r=r: e.tensor_tensor(
                        PT[:, r, q0:q0 + 128], PT[:, r, q0:q0 + 128], maskt[:, :], ALU.mult),
                        reads=[("PT", r), "maskt"], writes=[("PT", r)])
                if idx + LOOK < len(tiles):
                    issue_s(idx + LOOK)
                S.op("pe", lambda e, s=s, nk=nk, q0=q0, r=r, kt=kt, hh=hh, bo=bo: e.matmul(
                    psb[bo][:, q0:T], vbuf[0:nk, s, kt * 192 + hh * 64:kt * 192 + hh * 64 + 128],
                    PT[0:nk, r, q0:T], start=(kt == 0), stop=(kt == nkt - 1)),
                    reads=[("vbuf", s), ("PT", r)], writes=[("ps", bo)])
                if kt == nkt - 1:
                    r2 = nxt("oev", 2)
                    po = 64 * hh
                    pd = 64 - po
                    S.op("dve", lambda e, bo=bo, r2=r2, pd=pd: e.reciprocal(oev[pd:pd + 64, r2, :], psb[bo][pd:pd + 64, :]),
                         reads=[("ps", bo)], writes=[("oev", r2)])
                    S.op("dve", lambda e, bo=bo, r2=r2, po=po, pd=pd, hp=hp: e.tensor_tensor(
                        A[po:po + 64, 16 + hp, :], psb[bo][po:po + 64, :], oev[pd:pd + 64, r2, :], ALU.mult),
                        reads=[("oev", r2), ("ps", bo)], writes=[("A", 16 + hp)])
            for uo in range(2):
                slot = wneed(wo_units[uo])
                wv = wring[:, slot, :].rearrange("p (hp e) -> p hp e", hp=8)
                for jj in range(4):
                    j = uo * 4 + jj
                    bf = nxt("sbank", 5)

                    def mm(e, wv=wv, jj=jj, bf=bf):
                        ins = None
                        for hp in range(8):
                            ins = e.matmul(psb[bf][:, 0:T], wv[:, hp, jj * 128:(jj + 1) * 128], A[:, 16 + hp, 0:T],
                                           start=(hp == 0), stop=(hp == 7))
                        return ins
                    S.op("pe", mm, reads=wres(slot) + [("A", 16 + hp) for hp in range(8)], writes=[("ps", bf)])
                    S.op("dve", lambda e, j=j, bf=bf: e.scalar_tensor_tensor(h[:, j, 0:T], h[:, j, 0:T], ALPHA,
                                                                             psb[bf][:, 0:T], ALU.mult, ALU.add),
                         reads=[("h", j), ("ps", bf)], writes=[("h", j)])
            layer_norm(T, 1, 0)

        def write_out(i):
            def evac_store(tt, s, banks):
                S.op("act", lambda e: e.activation(xin[:, s, 0:512], psb[banks[0]][:, :], AF.Identity),
                     reads=[("ps", banks[0])], writes=[("xin", s)])
                S.op("dve", lambda e: e.tensor_copy(xin[:, s, 512:1024], psb[banks[1]][:, :]),
                     reads=[("ps", banks[1])], writes=[("xin", s, 1)])
                r0 = i * TS + tt * 128
                S.dma("sp", lambda e: e.dma_start(out=out_d[r0:r0 + 128, :], in_=xin[:, s, :]),
                      reads=[("xin", s), ("xin", s, 1)], writes=[("xin", s), ("xin", s, 1)])

            wb = {(tt, jb): bank() for tt in range(2) for jb in range(2)}
            for j in range(8):
                def trw(e, j=j):
                    ins = None
                    for tt in range(2):
                        ins = e.transpose(psb[wb[(tt, j // 4)]][:, (j % 4) * 128:(j % 4 + 1) * 128],
                                          h[:, j, tt * 128:(tt + 1) * 128], identf[:, :])
                    return ins
                S.op("pe", trw, reads=[("h", j), "identf"], writes=[("ps", wb[(tt, j // 4)]) for tt in range(2)])
            for tt in range(2):
                evac_store(tt, nxt("xin", 2), (wb[(tt, 0)], wb[(tt, 1)]))
            for tt in range(2, 4):
                s = nxt("xin", 2)
                banks = []
                for jb in range(2):
                    b = bank()
                    banks.append(b)

                    def tr(e, tt=tt, jb=jb, b=b):
                        ins = None
                        for jj in range(4):
                            ins = e.transpose(psb[b][:, jj * 128:(jj + 1) * 128],
                                              h[:, jb * 4 + jj, tt * 128:(tt + 1) * 128], identf[:, :])
                        return ins
                    S.op("pe", tr, reads=[("h", jb * 4 + jj) for jj in range(4)] + ["identf"], writes=[("ps", b)])
                evac_store(tt, s, banks)

        plan = []
        gu, dn = units_ffn(0)
        qk, vv = units_qkv()
        plan.append(("meta", gu, dn, qk, vv))
        for i in range(nsup):
            gu, dn = units_ffn(0)
            qk, vv = units_qkv()
            wo_u = units_wo()
            gu1, dn1 = units_ffn(1)
            plan.append((i, gu, dn, qk, vv, wo_u, gu1, dn1))

        _, gu, dn, qk, vv = plan[0]
        pool_layer(NMETA, [(meta_d, NMETA)])
        ffn(NMETA, 0, gu, dn)
        qkv(NMETA, [NMETA], 0, 0, qk, vv, want_q=False)
        for i in range(nsup):
            _, gu, dn, qk, vv, wo_u, gu1, dn1 = plan[1 + i]
            rows = [(x_d[i * TS + tt * 128:i * TS + (tt + 1) * 128, :], 128) for tt in range(4)]
            pool_layer(TS, rows, pre=(i > 0))
            ffn(TS, 0, gu, dn)
            qkv(TS, [128] * 4, 1 + 4 * i, NMETA + TS * i, qk, vv, want_q=True)
            attention(i, wo_u)
            if i + 1 < nsup:
                nsrc = x_d[(i + 1) * TS:(i + 1) * TS + 128, :]
                S.dma("sp", lambda e, nsrc=nsrc: e.dma_start(out=xpre[:, :], in_=nsrc), writes=["xpre"])
            ffn(TS, 1, gu1, dn1)
            write_out(i)
        S.build(st)
    return nc


_CACHE = {}


def kernel(x, meta_tokens, pool_w, pool_scale, fox_w_in, fox_b_f, fox_w_o,
           ffn_w_gate, ffn_w_up, ffn_w_down, ln_g, ln_b):
    f = lambda a: np.ascontiguousarray(np.asarray(a, dtype=np.float32))
    x = f(x)
    B = x.shape[0]
    if "nc" not in _CACHE:
        _CACHE["nc"] = build_program()
    nc = _CACHE["nc"]
    vecs = np.concatenate([f(ln_g).reshape(32, 128), f(ln_b).reshape(32, 128), f(pool_scale).reshape(8, 128)], axis=0)
    consts = make_consts()
    shared = {
        "meta": f(meta_tokens), "pool_w": f(pool_w)[0], "w_in": f(fox_w_in)[0], "w_o": f(fox_w_o)[0],
        "w_gate": f(ffn_w_gate), "w_up": f(ffn_w_up), "w_down": f(ffn_w_down),
        "vecs": np.ascontiguousarray(vecs),
        "bf_bc": np.ascontiguousarray(np.broadcast_to(f(fox_b_f).reshape(1, NH), (128, NH))),
    }
    shared.update(consts)
    in_maps = [dict(shared, x=x[b]) for b in range(B)]
    res = run_bass_kernel_spmd(nc, in_maps, core_ids=list(range(B)))
    return np.stack([np.asarray(r["out"], dtype=np.float32) for r in res.results], axis=0)
```

```python
import contextlib
import numpy as np
import concourse.bass as bass
import concourse.mybir as mybir
from concourse.bass_utils import run_bass_kernel_spmd

F32 = mybir.dt.float32
BF16 = mybir.dt.bfloat16
AF = mybir.ActivationFunctionType
ALU = mybir.AluOpType

D = 1024
SEQ = 4096
NMETA = 16
NH = 16
DFF = 2816
NFC = DFF // 128
TS = 512
NSUP = SEQ // TS
LTOT = SEQ + NMETA
NKT = 1 + SEQ // 128
ALPHA = float((2.0 * 2) ** 0.25)
EPS = 1e-5
POOL_WINDOWS = (2, 4, 8, 16)
NSLOT = 4
WAVE_FFN = True
WAVE_QKV = True
SLOT_ELEMS = 4096

ENGINES = ("pe", "act", "dve", "pool", "sp")


class Op:
    __slots__ = ("engine", "fn", "deps", "sig", "sig_index", "is_dma", "dsem", "dtarget")

    def __init__(self, engine, fn, is_dma):
        self.engine = engine
        self.fn = fn
        self.deps = []
        self.sig = False
        self.sig_index = 0
        self.is_dma = is_dma
        self.dsem = None
        self.dtarget = 0


class Sched:
    def __init__(self, nc, n_dma_sems=12):
        self.nc = nc
        self.ops = {e: [] for e in ENGINES}
        self.res = {}
        self.n_dma_sems = n_dma_sems
        self.dma_rr = 0
        self.dma_count = [0] * n_dma_sems
        self.dma_last = [None] * n_dma_sems
        self.n_sw = 0

    def _track(self, op, reads, writes):
        deps = {}
        for r in reads:
            st = self.res.get(r)
            if st is not None and st[0] is not None:
                deps[id(st[0])] = st[0]
        for r in writes:
            st = self.res.get(r)
            if st is not None:
                if st[0] is not None:
                    deps[id(st[0])] = st[0]
                for o in st[1]:
                    deps[id(o)] = o
        for r in reads:
            st = self.res.setdefault(r, [None, []])
            st[1].append(op)
        for r in writes:
            self.res[r] = [op, []]
        deps.pop(id(op), None)
        for d in deps.values():
            if d.engine == "pe" and op.engine == "pe" and not d.is_dma and not op.is_dma:
                continue
            op.deps.append(d)

    def op(self, engine, fn, reads=(), writes=()):
        o = Op(engine, fn, False)
        self._track(o, reads, writes)
        self.ops[engine].append(o)
        return o

    def dma(self, queue, fn, reads=(), writes=()):
        o = Op(queue, fn, True)
        self._track(o, reads, writes)
        if queue == "pool":
            o.dsem = ("sw", self.n_sw)
            o.dtarget = 16
            self.n_sw += 1
            self.ops[queue].append(o)
            return o
        s = self.dma_rr
        self.dma_rr = (self.dma_rr + 1) % self.n_dma_sems
        prev = self.dma_last[s]
        if prev is not None:
            o.deps.append(prev)
        self.dma_count[s] += 16
        o.dsem = s
        o.dtarget = self.dma_count[s]
        self.dma_last[s] = o
        self.ops[queue].append(o)
        return o

    def build(self, st):
        nc = self.nc
        for e in ENGINES:
            for o in self.ops[e]:
                for d in o.deps:
                    d.sig = True
        for e in ENGINES:
            cnt = 0
            for o in self.ops[e]:
                if not o.is_dma and o.sig:
                    cnt += 1
                    o.sig_index = cnt
        esem = {e: st.enter_context(nc.semaphore("sem_" + e)) for e in ENGINES}
        dsem = {i: st.enter_context(nc.semaphore("dsem%d" % i)) for i in range(self.n_dma_sems)}
        for i in range(self.n_sw):
            dsem[("sw", i)] = st.enter_context(nc.semaphore("swsem%d" % i))
        block = st.enter_context(nc.Block())
        sched = self

        def emit(ename, eng):
            waited = {}
            for o in sched.ops[ename]:
                need = {}
                for d in o.deps:
                    if d.is_dma:
                        key = ("d", d.dsem)
                        val = d.dtarget
                    else:
                        key = ("e", d.engine)
                        val = d.sig_index
                    if val > need.get(key, 0):
                        need[key] = val
                for key, val in need.items():
                    if waited.get(key, 0) >= val:
                        continue
                    waited[key] = val
                    sem = dsem[key[1]] if key[0] == "d" else esem[key[1]]
                    eng.wait_ge(sem, val)
                ins = o.fn(eng)
                if o.is_dma:
                    ins.then_inc(dsem[o.dsem], 16)
                elif o.sig:
                    ins.then_inc(esem[ename], 1)
            if ename == "sp":
                for i in range(sched.n_dma_sems):
                    if sched.dma_count[i] > waited.get(("d", i), 0):
                        eng.wait_ge(dsem[i], sched.dma_count[i])
                for i in range(sched.n_sw):
                    if waited.get(("d", ("sw", i)), 0) < 16:
                        eng.wait_ge(dsem[("sw", i)], 16)

        @block.sync
        def _(eng):
            emit("sp", eng)

        @block.tensor
        def _(eng):
            emit("pe", eng)

        @block.scalar
        def _(eng):
            emit("act", eng)

        @block.vector
        def _(eng):
            emit("dve", eng)

        @block.gpsimd
        def _(eng):
            emit("pool", eng)


def _bf16_round(a):
    a = np.asarray(a, np.float32)
    u = a.view(np.uint32).astype(np.uint64)
    u = (u + 0x7FFF + ((u >> 16) & 1)) & 0xFFFF0000
    return u.astype(np.uint32).view(np.float32)


def make_consts():
    c = {}
    c["identf"] = np.eye(128, dtype=np.float32)
    s = np.arange(128)[:, None]
    t = np.arange(128)[None, :]
    atd = np.zeros((128, 4, 128), np.float32)
    atoff = np.zeros((128, 4, 128), np.float32)
    atm = np.zeros((16, 4, 16), np.float32)
    for g, w in enumerate(POOL_WINDOWS):
        d = t - s
        atd[:, g, :] = np.where((d >= 0) & (d < w), 1.0 / w, 0.0) - (s == t)
        d2 = t - s + 128
        atoff[:, g, :] = np.where(d2 < w, 1.0 / w, 0.0)
        sm = np.arange(16)[:, None]
        tm = np.arange(16)[None, :]
        dm = tm - sm
        cnt = np.minimum(tm + 1, w).astype(np.float32)
        atm[:, g, :] = np.where((dm >= 0) & (dm < w), 1.0 / cnt, 0.0) - (sm == tm)
    c["atd"] = atd
    c["atoff"] = atoff
    c["atoffm"] = np.ascontiguousarray(atoff[112:128])
    hi = _bf16_round(atm)
    c["atmhi"] = hi
    c["atmlo"] = (atm - hi).astype(np.float32)
    selc = np.zeros((128, 4, 128), np.float32)
    selc[:, 0, :] = (s <= t)
    selc[127, 1, :] = 1.0
    selc[15, 2, :] = 1.0
    selc[0, 3, :] = 1.0
    c["selc"] = selc
    c["maskt"] = (s <= t).astype(np.float32)
    return c


CONST_SHAPES = {
    "identf": [128, 128], "atd": [128, 4, 128], "atoff": [128, 4, 128], "atoffm": [16, 4, 128],
    "atmhi": [16, 4, 16], "atmlo": [16, 4, 16], "selc": [128, 4, 128], "maskt": [128, 128],
}


def build_program(nsup=NSUP):
    nc = bass.Bass("TRN2", target_bir_lowering=False)
    din = lambda name, shape: nc.dram_tensor(name, shape, F32, kind="ExternalInput").ap()
    x_d = din("x", [SEQ, D])
    meta_d = din("meta", [NMETA, D])
    poolw_d = din("pool_w", [4, 256, 256])
    win_d = din("w_in", [D, 3 * D + NH])
    wo_d = din("w_o", [D, D])
    wg_d = din("w_gate", [2, D, DFF])
    wu_d = din("w_up", [2, D, DFF])
    wd_d = din("w_down", [2, DFF, D])
    vecs_d = din("vecs", [72, 128])
    bfbc_d = din("bf_bc", [128, NH])
    cst_d = {k: din(k, v) for k, v in CONST_SHAPES.items()}
    out_d = nc.dram_tensor("out", [SEQ, D], F32, kind="ExternalOutput").ap()
    kscr = nc.dram_tensor("kscr", [8, 128, LTOT], BF16).ap()
    vscr = nc.dram_tensor("vscr", [8, 128, NKT, 192], BF16).ap()
    wscr = nc.dram_tensor("wscr", [46, 128, SLOT_ELEMS], BF16).ap()

    S = Sched(nc)
    with contextlib.ExitStack() as st:
        sb = lambda name, shape, dt: st.enter_context(nc.sbuf_tensor(name, shape, dt))
        h = sb("h", [128, 8, TS], F32)
        hb = sb("hb", [128, 8, TS], BF16)
        A = sb("A", [128, 24, TS], BF16)
        xin = sb("xin", [128, 2, D], F32)
        xbr = sb("xbr", [128, 3, D], BF16)
        xpre = sb("xpre", [128, D], F32)
        zb = sb("zb", [128, 2, TS], BF16)
        zq = sb("zq", [128, 2, TS], BF16)
        msb = sb("msb", [128, TS], F32)
        m2 = sb("m2", [128, TS], F32)
        rstd = sb("rstd", [128, TS], F32)
        nmr = sb("nmr", [128, TS], F32)
        tmp = sb("tmp", [128, 2, TS], F32)
        sg = sb("sg", [128, 2, TS], F32)
        PT = sb("PT", [128, 3, TS], BF16)
        oev = sb("oev", [128, 2, TS], F32)
        KTs = sb("KTs", [128, 8, TS], BF16)
        Vs = sb("Vs", [128, 4, 8 * 192], BF16)
        kbuf = sb("kbuf", [128, 2, LTOT], BF16)
        vbuf = sb("vbuf", [128, 2, NKT * 192], BF16)
        cpos = sb("cpos", [128, NKT, NH], F32)
        biasT = sb("biasT", [128, NKT, NH], F32)
        cref = sb("cref", [128, NH], F32)
        xf = sb("xf", [128, 2, NH], F32)
        wring = sb("wring", [128, NSLOT, SLOT_ELEMS], BF16)
        pw = sb("pw", [128, 4, 2, 256], BF16)
        wf = sb("wf", [128, 8, NH], BF16)
        vecs_sb = sb("vecs_sb", [72, 128], F32)
        vecT = sb("vecT", [128, 72], F32)
        bfbc = sb("bfbc", [128, NH], F32)
        identf = sb("identf_sb", [128, 128], F32)
        atd = sb("atd_sb", [128, 4, 128], BF16)
        atoff = sb("atoff_sb", [128, 4, 128], BF16)
        atoffm = sb("atoffm_sb", [16, 4, 128], BF16)
        atmhi = sb("atmhi_sb", [16, 4, 16], BF16)
        atmlo = sb("atmlo_sb", [16, 4, 16], BF16)
        selc = sb("selc_sb", [128, 4, 128], F32)
        maskt = sb("maskt_sb", [128, 128], BF16)
        onesf = sb("onesf", [128, 64], F32)
        onesm = sb("onesm", [128, 128], BF16)
        psb = [st.enter_context(nc.psum_tensor("ps%d" % i, [128, 512], F32)) for i in range(8)]

        rot = {"b": 0}

        def bank():
            b = rot["b"]
            rot["b"] = (b + 1) % 8
            return b

        ring = {}

        def nxt(name, n):
            v = ring.get(name, 0)
            ring[name] = (v + 1) % n
            return v

        S.dma("sp", lambda e: e.dma_start(out=identf[:], in_=cst_d["identf"]), writes=["identf"])
        S.dma("sp", lambda e: e.dma_start(out=selc[:], in_=cst_d["selc"]), writes=["selc"])
        S.dma("sp", lambda e: e.dma_start(out=vecs_sb[:], in_=vecs_d), writes=["vecs_sb"])
        S.dma("sp", lambda e: e.dma_start(out=bfbc[:], in_=bfbc_d), writes=["bfbc"])
        for name, t_ in (("atd", atd), ("atoff", atoff), ("atoffm", atoffm), ("atmhi", atmhi),
                         ("atmlo", atmlo), ("maskt", maskt)):
            S.dma("pool", lambda e, t_=t_, name=name: e.dma_start(out=t_[:], in_=cst_d[name]), writes=[name])
        S.dma("pool", lambda e: e.dma_start(
            out=pw[:], in_=poolw_d.rearrange("g (cc p) e -> p g cc e", p=128)), writes=["pw"])
        S.dma("pool", lambda e: e.dma_start(
            out=wf[:], in_=win_d.rearrange("(j p) c -> p j c", p=128)[:, :, 3 * D:3 * D + NH]), writes=["wf"])
        S.op("dve", lambda e: e.memset(onesf[:], 1.0), writes=["onesf"])
        S.op("dve", lambda e: e.memset(onesm[:], 1.0 / 1024.0), writes=["onesm"])
        S.op("dve", lambda e: e.memset(Vs[:], 0.0), writes=[("Vs", t) for t in range(4)])
        S.op("dve", lambda e: e.memset(Vs[:].rearrange("p t (hp c) -> p t hp c", hp=8)[:, :, :, 64:128], 1.0),
             writes=[("Vs", t) for t in range(4)])
        S.op("dve", lambda e: e.memset(cpos[:], 0.0), writes=[("cpos", k) for k in range(NKT)])
        b0 = bank()
        S.op("pe", lambda e: e.transpose(psb[b0][:, 0:72], vecs_sb[:, :], identf[0:72, 0:72]),
             reads=["vecs_sb", "identf"], writes=[("ps", b0)])
        S.op("dve", lambda e: e.tensor_copy(vecT[:], psb[b0][:, 0:72]), reads=[("ps", b0)], writes=["vecT"])

        def gvec(layer, sub, j):
            k = (layer * 2 + sub) * 8 + j
            return vecT[:, k:k + 1]

        def bvec(layer, sub, j):
            k = 32 + (layer * 2 + sub) * 8 + j
            return vecT[:, k:k + 1]

        def psvec(j):
            return vecT[:, 64 + j:65 + j]

        units = []

        def add_unit(parts, key, n):
            units.append((parts, key, n))
            return len(units) - 1

        wstate = {"issued": 0}
        wkeys = {}

        def wneed(u, base=None):
            lim = min(len(units), (u if base is None else base) + NSLOT)
            while wstate["issued"] < lim:
                v = wstate["issued"]
                slot = v % NSLOT
                parts, key, n = units[v]
                if key in wkeys:
                    kid = wkeys[key]
                    S.dma("sp", lambda e, slot=slot, kid=kid, n=n: e.dma_start(out=wring[:, slot, 0:n],
                                                                             in_=wscr[kid][:, 0:n]),
                          reads=[("wscr", kid)], writes=[("w", slot, 0), ("w", slot, 1)])
                else:
                    kid = len(wkeys)
                    wkeys[key] = kid
                    for k, (dst_fn, src) in enumerate(parts):
                        dst = dst_fn(wring[:, slot, :])
                        S.dma("pool", lambda e, dst=dst, src=src: e.dma_start(out=dst, in_=src),
                              writes=[("w", slot, k)])
                    S.dma("sp", lambda e, slot=slot, kid=kid, n=n: e.dma_start(out=wscr[kid][:, 0:n],
                                                                             in_=wring[:, slot, 0:n]),
                          reads=[("w", slot, 0), ("w", slot, 1)], writes=[("wscr", kid)])
                wstate["issued"] += 1
            return u % NSLOT

        def wres(slot):
            return [("w", slot, 0), ("w", slot, 1)]

        def units_ffn(l):
            gu = []
            for ug in range(NFC // 2):
                c0 = ug * 2
                parts = []
                for k, wsrc in enumerate((wg_d, wu_d)):
                    src = wsrc[l].rearrange("(j p) c -> p j c", p=128)[:, :, c0 * 128:c0 * 128 + 256]
                    parts.append((lambda sl, k=k: sl.rearrange("p (a j c) -> p a j c", a=2, j=8)[:, k], src))
                gu.append(add_unit(parts, ("gu", l, ug), SLOT_ELEMS))
            dn = []
            for j in range(8):
                src = wd_d[l].rearrange("(c p) e -> p c e", p=128)[:, :, j * 128:(j + 1) * 128]
                dn.append(add_unit([(lambda sl: sl[:, 0:NFC * 128].rearrange("p (c e) -> p c e", c=NFC), src)],
                                   ("dn", l, j), NFC * 128))
            return gu, dn

        def units_qkv():
            qk = []
            for uq in range(4):
                parts = []
                for k in range(2):
                    src = win_d.rearrange("(j p) c -> p j c", p=128)[:, :, k * D + uq * 256:k * D + uq * 256 + 256]
                    parts.append((lambda sl, k=k: sl.rearrange("p (a j c) -> p a j c", a=2, j=8)[:, k], src))
                qk.append(add_unit(parts, ("qk", uq), SLOT_ELEMS))
            vv = []
            for hf in range(2):
                src = win_d.rearrange("(j p) c -> p j c", p=128)[:, :, 2 * D + hf * 512:2 * D + hf * 512 + 512]
                vv.append(add_unit([(lambda sl: sl.rearrange("p (j c) -> p j c", j=8), src)], ("v", hf), SLOT_ELEMS))
            return qk, vv

        def units_wo():
            res = []
            for uo in range(2):
                src = wo_d.rearrange("(hp p) e -> p hp e", p=128)[:, :, uo * 512:uo * 512 + 512]
                res.append(add_unit([(lambda sl: sl.rearrange("p (hp e) -> p hp e", hp=8), src)], ("wo", uo), SLOT_ELEMS))
            return res

        def layer_norm(T, layer, sub):
            bm = bank()
            bq = bank()
            for j in range(8):
                r = nxt("zb", 2)
                S.op("act", lambda e, j=j, r=r: e.activation(zb[:, r, 0:T], h[:, j, 0:T], AF.Identity),
                     reads=[("h", j)], writes=[("zb", r)])
                S.op("act", lambda e, j=j, r=r: e.activation(zq[:, r, 0:T], h[:, j, 0:T], AF.Square),
                     reads=[("h", j)], writes=[("zq", r)])
                S.op("pe", lambda e, j=j, r=r: e.matmul(psb[bm][:, 0:T], onesm[:, :], zb[:, r, 0:T],
                                                       start=(j == 0), stop=(j == 7)),
                     reads=[("zb", r), "onesm"], writes=[("ps", bm)])
                S.op("pe", lambda e, j=j, r=r: e.matmul(psb[bq][:, 0:T], onesm[:, :], zq[:, r, 0:T],
                                                       start=(j == 0), stop=(j == 7)),
                     reads=[("zq", r), "onesm"], writes=[("ps", bq)])
            S.op("act", lambda e: e.activation(msb[:, 0:T], psb[bm][:, 0:T], AF.Identity),
                 reads=[("ps", bm)], writes=["msb"])
            S.op("act", lambda e: e.activation(m2[:, 0:T], psb[bm][:, 0:T], AF.Square),
                 reads=[("ps", bm)], writes=["m2"])
            S.op("dve", lambda e: e.tensor_tensor(m2[:, 0:T], psb[bq][:, 0:T], m2[:, 0:T], ALU.subtract),
                 reads=[("ps", bq), "m2"], writes=["m2"])
            S.op("act", lambda e: e.activation(rstd[:, 0:T], m2[:, 0:T], AF.Ln, bias=EPS),
                 reads=["m2"], writes=["rstd"])
            S.op("act", lambda e: e.activation(rstd[:, 0:T], rstd[:, 0:T], AF.Exp, scale=-0.5),
                 reads=["rstd"], writes=["rstd"])
            S.op("dve", lambda e: e.scalar_tensor_tensor(nmr[:, 0:T], msb[:, 0:T], -1.0, rstd[:, 0:T],
                                                         ALU.mult, ALU.mult),
                 reads=["msb", "rstd"], writes=["nmr"])
            for j in range(8):
                r = nxt("tmp", 2)
                S.op("dve", lambda e, j=j, r=r: e.tensor_tensor(tmp[:, r, 0:T], h[:, j, 0:T], rstd[:, 0:T], ALU.mult),
                     reads=[("h", j), "rstd"], writes=[("tmp", r)])
                S.op("dve", lambda e, j=j, r=r: e.tensor_tensor(tmp[:, r, 0:T], tmp[:, r, 0:T], nmr[:, 0:T], ALU.add),
                     reads=[("tmp", r), "nmr"], writes=[("tmp", r)])
                S.op("act", lambda e, j=j, r=r: e.activation(h[:, j, 0:T], tmp[:, r, 0:T], AF.Identity,
                                                             bias=bvec(layer, sub, j), scale=gvec(layer, sub, j)),
                     reads=[("tmp", r), "vecT"], writes=[("h", j)])
                S.op("act", lambda e, j=j, r=r: e.activation(hb[:, j, 0:T], tmp[:, r, 0:T], AF.Identity,
                                                             bias=bvec(layer, sub, j), scale=gvec(layer, sub, j)),
                     reads=[("tmp", r), "vecT"], writes=[("hb", j)])

        def ffn(T, layer, gu, dn):
            if WAVE_FFN:
                s0 = wneed(gu[0])
                wv0 = wring[:, s0, :].rearrange("p (a j c) -> p a j c", a=2, j=8)
                wb = [[bank() for _ in range(2)] for _ in range(2)]
                for j in range(8):
                    def mmw(e, j=j):
                        ins = None
                        for c in range(2):
                            for k in range(2):
                                ins = e.matmul(psb[wb[c][k]][:, 0:T], wv0[:, k, j, c * 128:c * 128 + 128],
                                               hb[:, j, 0:T], start=(j == 0), stop=(j == 7))
                        return ins
                    S.op("pe", mmw, reads=wres(s0) + [("hb", j)],
                         writes=[("ps", wb[c][k]) for c in range(2) for k in range(2)])
                for c in range(2):
                    bg, bu = wb[c]
                    r = nxt("sg", 2)
                    S.op("act", lambda e, bg=bg, r=r: e.activation(sg[:, r, 0:T], psb[bg][:, 0:T], AF.Silu),
                         reads=[("ps", bg)], writes=[("sg", r)])
                    S.op("dve", lambda e, bu=bu, r=r, c=c: e.tensor_tensor(A[:, c, 0:T], sg[:, r, 0:T],
                                                                         psb[bu][:, 0:T], ALU.mult),
                         reads=[("sg", r), ("ps", bu)], writes=[("A", c)])
            for ug in range(1 if WAVE_FFN else 0, NFC // 2):
                slot = wneed(gu[ug])
                wv = wring[:, slot, :].rearrange("p (a j c) -> p a j c", a=2, j=8)
                for cc in range(2):
                    c = ug * 2 + cc
                    bg = bank()
                    bu = bank()

                    def mm(e, wv=wv, cc=cc, bg=bg, bu=bu):
                        ins = None
                        for k, bb in ((0, bg), (1, bu)):
                            for j in range(8):
                                ins = e.matmul(psb[bb][:, 0:T], wv[:, k, j, cc * 128:(cc + 1) * 128], hb[:, j, 0:T],
                                               start=(j == 0), stop=(j == 7))
                        return ins
                    S.op("pe", mm, reads=wres(slot) + [("hb", j) for j in range(8)],
                         writes=[("ps", bg), ("ps", bu)])
                    r = nxt("sg", 2)
                    S.op("act", lambda e, bg=bg, r=r: e.activation(sg[:, r, 0:T], psb[bg][:, 0:T], AF.Silu),
                         reads=[("ps", bg)], writes=[("sg", r)])
                    S.op("dve", lambda e, bu=bu, r=r, c=c: e.tensor_tensor(A[:, c, 0:T], sg[:, r, 0:T],
                                                                         psb[bu][:, 0:T], ALU.mult),
                         reads=[("sg", r), ("ps", bu)], writes=[("A", c)])
            for j in range(8):
                slot = wneed(dn[j])
                wv = wring[:, slot, 0:NFC * 128].rearrange("p (c e) -> p c e", c=NFC)
                bf = bank()

                def mm(e, wv=wv, bf=bf):
                    ins = None
                    for c in range(NFC):
                        ins = e.matmul(psb[bf][:, 0:T], wv[:, c, :], A[:, c, 0:T], start=(c == 0), stop=(c == NFC - 1))
                    return ins
                S.op("pe", mm, reads=wres(slot) + [("A", c) for c in range(NFC)], writes=[("ps", bf)])
                S.op("dve", lambda e, j=j, bf=bf: e.scalar_tensor_tensor(h[:, j, 0:T], h[:, j, 0:T], ALPHA,
                                                                         psb[bf][:, 0:T], ALU.mult, ALU.add),
                     reads=[("h", j), ("ps", bf)], writes=[("h", j)])
            layer_norm(T, layer, 1)

        prev_x = {"slot": None, "tp": 0}

        def pool_layer(T, src_rows, pre=False):
            nt = len(src_rows)
            for tt, (src, tp) in enumerate(src_rows):
                if pre and tt == 0:
                    xt_ = xpre[0:tp, :]
                    xkey = "xpre"
                else:
                    s = nxt("xin", 2)
                    S.dma("sp", lambda e, s=s, src=src, tp=tp: e.dma_start(out=xin[0:tp, s, :], in_=src),
                          writes=[("xin", s)])
                    xt_ = xin[0:tp, s, :]
                    xkey = ("xin", s)
                r = nxt("xbr", 3)
                S.op("act", lambda e, xt_=xt_, r=r, tp=tp: e.activation(xbr[0:tp, r, :], xt_, AF.Identity),
                     reads=[xkey], writes=[("xbr", r)])
                for jb in range(2):
                    b = bank()

                    def tr(e, xt_=xt_, tp=tp, jb=jb, b=b):
                        ins = None
                        for jj in range(4):
                            j = jb * 4 + jj
                            ins = e.transpose(psb[b][:, jj * 128:jj * 128 + tp], xt_[:, j * 128:(j + 1) * 128],
                                              identf[0:tp, 0:tp])
                        return ins
                    S.op("pe", tr, reads=[xkey, "identf"], writes=[("ps", b)])
                    S.op("act", lambda e, tp=tp, jb=jb, b=b, tt=tt: e.activation(
                        h[:, jb * 4:jb * 4 + 4, tt * 128:tt * 128 + tp],
                        psb[b][:, :].rearrange("p (a c) -> p a c", a=4)[:, :, 0:tp], AF.Identity, scale=ALPHA),
                        reads=[("ps", b)], writes=[("h", jb * 4 + jj) for jj in range(4)])
                meta_tile = (tp == NMETA)
                pslot, ptp = prev_x["slot"], prev_x["tp"]
                for jb in range(2):
                    b = bank()

                    def bm(e, r=r, tp=tp, jb=jb, b=b, meta_tile=meta_tile, pslot=pslot, ptp=ptp):
                        ins = None
                        for jj in range(4):
                            j = jb * 4 + jj
                            g = j // 2
                            o = psb[b][:, jj * 128:jj * 128 + tp]
                            xl = xbr[0:tp, r, j * 128:(j + 1) * 128]
                            if meta_tile:
                                e.matmul(o, xl, atmhi[0:tp, g, 0:tp], start=True, stop=False)
                                ins = e.matmul(o, xl, atmlo[0:tp, g, 0:tp], start=False, stop=True)
                            else:
                                e.matmul(o, xl, atd[:, g, :], start=True, stop=False)
                                xp = xbr[0:ptp, pslot, j * 128:(j + 1) * 128]
                                rhs = atoffm[0:16, g, :] if ptp == NMETA else atoff[:, g, :]
                                ins = e.matmul(o, xp, rhs, start=False, stop=True)
                        return ins
                    rd = [("xbr", r), "atd", "atoff", "atoffm", "atmhi", "atmlo"]
                    if pslot is not None:
                        rd.append(("xbr", pslot))
                    S.op("pe", bm, reads=rd, writes=[("ps", b)])
                    S.op("dve", lambda e, tp=tp, jb=jb, b=b, tt=tt: e.tensor_copy(
                        A[:, jb * 4:jb * 4 + 4, tt * 128:tt * 128 + tp],
                        psb[b][:, :].rearrange("p (a c) -> p a c", a=4)[:, :, 0:tp]),
                        reads=[("ps", b)], writes=[("A", jb * 4 + jj) for jj in range(4)])
                prev_x["slot"], prev_x["tp"] = r, tp
            for je in range(8):
                g = je // 2
                b = bank()

                def mm(e, je=je, g=g, b=b):
                    e.matmul(psb[b][:, 0:T], pw[:, g, 0, (je % 2) * 128:(je % 2) * 128 + 128], A[:, 2 * g, 0:T],
                             start=True, stop=False)
                    return e.matmul(psb[b][:, 0:T], pw[:, g, 1, (je % 2) * 128:(je % 2) * 128 + 128],
                                    A[:, 2 * g + 1, 0:T], start=False, stop=True)
                S.op("pe", mm, reads=["pw", ("A", 2 * g), ("A", 2 * g + 1)], writes=[("ps", b)])
                S.op("dve", lambda e, je=je, b=b: e.scalar_tensor_tensor(h[:, je, 0:T], psb[b][:, 0:T], psvec(je),
                                                                         h[:, je, 0:T], ALU.mult, ALU.add),
                     reads=[("ps", b), ("h", je), "vecT"], writes=[("h", je)])
            layer_norm(T, 0, 0)

        def qkv(T, tps, kt0, tok0, qk, vv, want_q):
            if want_q:
                for hd in range(NH):
                    zp = 64 if hd % 2 == 0 else 0
                    S.op("dve", lambda e, hd=hd, zp=zp: e.memset(A[zp:zp + 64, hd, :], 0.0),
                         writes=[("A", hd)])
            ks = (0, 1) if want_q else (1,)

            def qk_evac(hp, k, b):
                if k == 0:
                    S.op("act", lambda e: e.activation(A[0:64, 2 * hp, 0:T], psb[b][0:64, 0:T],
                                                       AF.Identity, scale=0.125),
                         reads=[("ps", b)], writes=[("A", 2 * hp)])
                    S.op("act", lambda e: e.activation(A[64:128, 2 * hp + 1, 0:T],
                                                       psb[b][64:128, 0:T], AF.Identity, scale=0.125),
                         reads=[("ps", b)], writes=[("A", 2 * hp + 1)])
                else:
                    S.op("dve", lambda e: e.tensor_copy(KTs[:, hp, 0:T], psb[b][:, 0:T]),
                         reads=[("ps", b)], writes=[("KTs", hp)])

            s0 = wneed(qk[0])
            s1 = wneed(qk[1], base=qk[0])
            wvs = [wring[:, sl, :].rearrange("p (a j c) -> p a j c", a=2, j=8) for sl in (s0, s1)]
            wb = {(hp, k): bank() for hp in range(4) for k in ks}
            for j in (range(8) if WAVE_QKV else ()):
                def mmw(e, j=j):
                    ins = None
                    for hp in range(4):
                        for k in ks:
                            ins = e.matmul(psb[wb[(hp, k)]][:, 0:T],
                                           wvs[hp // 2][:, k, j, (hp % 2) * 128:(hp % 2) * 128 + 128],
                                           hb[:, j, 0:T], start=(j == 0), stop=(j == 7))
                    return ins
                S.op("pe", mmw, reads=wres(s0) + wres(s1) + [("hb", j)], writes=[("ps", b_) for b_ in wb.values()])
            for hp in (range(4) if WAVE_QKV else ()):
                for k in ks:
                    qk_evac(hp, k, wb[(hp, k)])
            for uq in range(2 if WAVE_QKV else 0, 4):
                slot = wneed(qk[uq])
                wv = wring[:, slot, :].rearrange("p (a j c) -> p a j c", a=2, j=8)
                for hpp in range(2):
                    hp = uq * 2 + hpp
                    for k in ks:
                        b = bank()

                        def mm(e, wv=wv, k=k, hpp=hpp, b=b):
                            ins = None
                            for j in range(8):
                                ins = e.matmul(psb[b][:, 0:T], wv[:, k, j, hpp * 128:(hpp + 1) * 128], hb[:, j, 0:T],
                                               start=(j == 0), stop=(j == 7))
                            return ins
                        S.op("pe", mm, reads=wres(slot) + [("hb", j) for j in range(8)], writes=[("ps", b)])
                        qk_evac(hp, k, b)
            for hf in range(2):
                slot = wneed(vv[hf])
                wv = wring[:, slot, :].rearrange("p (j c) -> p j c", j=8)
                for tt, tp in enumerate(tps):
                    b = bank()

                    def mm(e, wv=wv, tt=tt, tp=tp, b=b):
                        ins = None
                        for j in range(8):
                            ins = e.matmul(psb[b][0:tp, :], hb[:, j, tt * 128:tt * 128 + tp], wv[:, j, :],
                                           start=(j == 0), stop=(j == 7))
                        return ins
                    S.op("pe", mm, reads=wres(slot) + [("hb", j) for j in range(8)], writes=[("ps", b)])
                    S.op("act", lambda e, tt=tt, tp=tp, b=b, hf=hf: e.activation(
                        Vs[0:tp, tt, :].rearrange("p (hp a c) -> p hp a c", hp=8, a=3)[:, hf * 4:hf * 4 + 4, 0:3:2, :],
                        psb[b][0:tp, :].rearrange("p (hp a c) -> p hp a c", hp=4, a=2), AF.Identity),
                        reads=[("ps", b), ("Vs", tt)], writes=[("Vs", tt, hf)])
            for tt, tp in enumerate(tps):
                kt = kt0 + tt
                b = bank()

                def mm(e, tt=tt, tp=tp, b=b):
                    ins = None
                    for j in range(8):
                        ins = e.matmul(psb[b][0:tp, 0:NH], hb[:, j, tt * 128:tt * 128 + tp], wf[:, j, :],
                                       start=(j == 0), stop=(j == 7))
                    return ins
                S.op("pe", mm, reads=["wf"] + [("hb", j) for j in range(8)], writes=[("ps", b)])
                r = nxt("xf", 2)
                S.op("dve", lambda e, tp=tp, b=b, r=r: e.tensor_tensor(xf[0:tp, r, :], psb[b][0:tp, 0:NH], bfbc[0:tp, :],
                                                                       ALU.add),
                     reads=[("ps", b), "bfbc"], writes=[("xf", r)])
                S.op("act", lambda e, tp=tp, r=r: e.activation(xf[0:tp, r, :], xf[0:tp, r, :], AF.Exp, scale=-1.0),
                     reads=[("xf", r)], writes=[("xf", r)])
                S.op("act", lambda e, tp=tp, r=r: e.activation(xf[0:tp, r, :], xf[0:tp, r, :], AF.Ln, bias=1.0),
                     reads=[("xf", r)], writes=[("xf", r)])
                b2 = bank()

                def cs(e, tp=tp, r=r, b2=b2, kt=kt):
                    first = (kt == 0)
                    ins = e.matmul(psb[b2][0:tp, 0:NH], selc[0:tp, 0, 0:tp], xf[0:tp, r, :], start=True, stop=first)
                    if not first:
                        if kt == 1:
                            ins = e.matmul(psb[b2][0:tp, 0:NH], selc[0:16, 2, 0:tp], cpos[0:16, 0, :],
                                           start=False, stop=True)
                        else:
                            ins = e.matmul(psb[b2][0:tp, 0:NH], selc[:, 1, 0:tp], cpos[:, kt - 1, :],
                                           start=False, stop=True)
                    return ins
                rd = [("xf", r), "selc"]
                if kt > 0:
                    rd.append(("cpos", kt - 1))
                S.op("pe", cs, reads=rd, writes=[("ps", b2)])
                S.op("dve", lambda e, tp=tp, b2=b2, kt=kt: e.tensor_copy(cpos[0:tp, kt, :], psb[b2][0:tp, 0:NH]),
                     reads=[("ps", b2)], writes=[("cpos", kt)])
            S.dma("sp", lambda e: e.dma_start(
                out=kscr.rearrange("hp p t -> p hp t")[:, :, tok0:tok0 + T], in_=KTs[:, :, 0:T]),
                reads=[("KTs", hp) for hp in range(8)], writes=[("kscr", tok0)])
            for tt, tp in enumerate(tps):
                kt = kt0 + tt
                S.dma("sp", lambda e, tt=tt, tp=tp, kt=kt: e.dma_start(
                    out=vscr.rearrange("hp p k c -> p hp k c")[:, :, kt, :],
                    in_=Vs[:, tt, :].rearrange("p (hp c) -> p hp c", hp=8)),
                    reads=[("Vs", tt, 0), ("Vs", tt, 1), ("Vs", tt)], writes=[("vscr", kt)])

        def attention(i, wo_units):
            T = TS
            nkt = 1 + 4 * (i + 1)
            nkeys = NMETA + TS * (i + 1)
            ktref = 4 * i + 1 + 2
            b = bank()
            S.op("pe", lambda e: e.matmul(psb[b][:, 0:NH], selc[:, 3, :], cpos[:, ktref, :], start=True, stop=True),
                 reads=["selc", ("cpos", ktref)], writes=[("ps", b)])
            S.op("dve", lambda e: e.tensor_copy(cref[:, :], psb[b][:, 0:NH]), reads=[("ps", b)], writes=["cref"])
            S.op("dve", lambda e: e.tensor_tensor(biasT[:, 0:nkt, :], cpos[:, 0:nkt, :],
                                                  cref[:, :].unsqueeze(1).broadcast_to([128, nkt, NH]), ALU.subtract),
                 reads=["cref"] + [("cpos", k) for k in range(nkt)], writes=["biasT"])
            kv_res = [("kscr", t) for t in [0] + [NMETA + TS * q for q in range(i + 1)]]
            v_res = [("vscr", k) for k in range(nkt)]

            def load_kv(hp):
                s = hp % 2
                S.dma("sp", lambda e, hp=hp, s=s: e.dma_start(out=kbuf[:, s, 0:nkeys], in_=kscr[hp][:, 0:nkeys]),
                      reads=kv_res, writes=[("kbuf", s)])
                S.dma("sp", lambda e, hp=hp, s=s: e.dma_start(
                    out=vbuf[:, s, 0:nkt * 192], in_=vscr[hp][:, 0:nkt, :].rearrange("p k c -> p (k c)")),
                    reads=v_res, writes=[("vbuf", s)])
            LOOK = 3
            tiles = []
            for hd in range(NH):
                for kt in range(nkt):
                    tiles.append((hd, kt))
            sb_of = {}
            pending = []

            def issue_s(idx):
                hd, kt = tiles[idx]
                hp, hh = hd // 2, hd % 2
                s = hp % 2
                p0 = 64 * hh
                nk = NMETA if kt == 0 else 128
                kc0 = 0 if kt == 0 else NMETA + 128 * (kt - 1)
                dg = kt - (4 * i + 1)
                q0 = 128 * dg if dg > 0 else 0
                bs = nxt("sbank", 5)
                sb_of[idx] = bs
                S.op("pe", lambda e: e.matmul(
                    psb[bs][0:nk, q0:T], kbuf[:, s, kc0:kc0 + nk], A[:, hd, q0:T],
                    start=True, stop=True),
                    reads=[("kbuf", s), ("A", hd)], writes=[("ps", bs)])

            load_kv(0)
            for idx in range(min(LOOK, len(tiles))):
                issue_s(idx)
            for idx, (hd, kt) in enumerate(tiles):
                hp, hh = hd // 2, hd % 2
                s = hp % 2
                if kt == 0 and hh == 0 and hp + 1 < 8:
                    load_kv(hp + 1)
                bo = 6 + (hd % 2)
                nk = NMETA if kt == 0 else 128
                dg = kt - (4 * i + 1)
                q0 = 128 * dg if dg > 0 else 0
                bs = sb_of.pop(idx)
                r = nxt("PT", 3)
                S.op("act", lambda e, nk=nk, q0=q0, bs=bs, r=r, kt=kt, hd=hd: e.activation(
                    PT[0:nk, r, q0:T], psb[bs][0:nk, q0:T], AF.Exp, bias=biasT[0:nk, kt, hd:hd + 1]),
                    reads=[("ps", bs), "biasT"], writes=[("PT", r)])
                if dg >= 0:
                    S.op("dve", lambda e, q0=q0, r=r: e.tensor_tensor(
                        PT[:, r, q0:q0 + 128], PT[:, r, q0:q0 + 128], maskt[:, :], ALU.mult),
                        reads=[("PT", r), "maskt"], writes=[("PT", r)])
                if idx + LOOK < len(tiles):
                    issue_s(idx + LOOK)
                S.op("pe", lambda e, s=s, nk=nk, q0=q0, r=r, kt=kt, hh=hh, bo=bo: e.matmul(
                    psb[bo][:, q0:T], vbuf[0:nk, s, kt * 192 + hh * 64:kt * 192 + hh * 64 + 128],
                    PT[0:nk, r, q0:T], start=(kt == 0), stop=(kt == nkt - 1)),
                    reads=[("vbuf", s), ("PT", r)], writes=[("ps", bo)])
                if kt == nkt - 1:
                    r2 = nxt("oev", 2)
                    po = 64 * hh
                    pd = 64 - po
                    S.op("dve", lambda e, bo=bo, r2=r2, pd=pd: e.reciprocal(oev[pd:pd + 64, r2, :], psb[bo][pd:pd + 64, :]),
                         reads=[("ps", bo)], writes=[("oev", r2)])
                    S.op("dve", lambda e, bo=bo, r2=r2, po=po, pd=pd, hp=hp: e.tensor_tensor(
                        A[po:po + 64, 16 + hp, :], psb[bo][po:po + 64, :], oev[pd:pd + 64, r2, :], ALU.mult),
                        reads=[("oev", r2), ("ps", bo)], writes=[("A", 16 + hp)])
            for uo in range(2):
                slot = wneed(wo_units[uo])
                wv = wring[:, slot, :].rearrange("p (hp e) -> p hp e", hp=8)
                for jj in range(4):
                    j = uo * 4 + jj
                    bf = nxt("sbank", 5)

                    def mm(e, wv=wv, jj=jj, bf=bf):
                        ins = None
                        for hp in range(8):
                            ins = e.matmul(psb[bf][:, 0:T], wv[:, hp, jj * 128:(jj + 1) * 128], A[:, 16 + hp, 0:T],
                                           start=(hp == 0), stop=(hp == 7))
                        return ins
                    S.op("pe", mm, reads=wres(slot) + [("A", 16 + hp) for hp in range(8)], writes=[("ps", bf)])
                    S.op("dve", lambda e, j=j, bf=bf: e.scalar_tensor_tensor(h[:, j, 0:T], h[:, j, 0:T], ALPHA,
                                                                             psb[bf][:, 0:T], ALU.mult, ALU.add),
                         reads=[("h", j), ("ps", bf)], writes=[("h", j)])
            layer_norm(T, 1, 0)

        def write_out(i):
            def evac_store(tt, s, banks):
                S.op("act", lambda e: e.activation(xin[:, s, 0:512], psb[banks[0]][:, :], AF.Identity),
                     reads=[("ps", banks[0])], writes=[("xin", s)])
                S.op("dve", lambda e: e.tensor_copy(xin[:, s, 512:1024], psb[banks[1]][:, :]),
                     reads=[("ps", banks[1])], writes=[("xin", s, 1)])
                r0 = i * TS + tt * 128
                S.dma("sp", lambda e: e.dma_start(out=out_d[r0:r0 + 128, :], in_=xin[:, s, :]),
                      reads=[("xin", s), ("xin", s, 1)], writes=[("xin", s), ("xin", s, 1)])

            wb = {(tt, jb): bank() for tt in range(2) for jb in range(2)}
            for j in range(8):
                def trw(e, j=j):
                    ins = None
                    for tt in range(2):
                        ins = e.transpose(psb[wb[(tt, j // 4)]][:, (j % 4) * 128:(j % 4 + 1) * 128],
                                          h[:, j, tt * 128:(tt + 1) * 128], identf[:, :])
                    return ins
                S.op("pe", trw, reads=[("h", j), "identf"], writes=[("ps", wb[(tt, j // 4)]) for tt in range(2)])
            for tt in range(2):
                evac_store(tt, nxt("xin", 2), (wb[(tt, 0)], wb[(tt, 1)]))
            for tt in range(2, 4):
                s = nxt("xin", 2)
                banks = []
                for jb in range(2):
                    b = bank()
                    banks.append(b)

                    def tr(e, tt=tt, jb=jb, b=b):
                        ins = None
                        for jj in range(4):
                            ins = e.transpose(psb[b][:, jj * 128:(jj + 1) * 128],
                                              h[:, jb * 4 + jj, tt * 128:(tt + 1) * 128], identf[:, :])
                        return ins
                    S.op("pe", tr, reads=[("h", jb * 4 + jj) for jj in range(4)] + ["identf"], writes=[("ps", b)])
                evac_store(tt, s, banks)

        plan = []
        gu, dn = units_ffn(0)
        qk, vv = units_qkv()
        plan.append(("meta", gu, dn, qk, vv))
        for i in range(nsup):
            gu, dn = units_ffn(0)
            qk, vv = units_qkv()
            wo_u = units_wo()
            gu1, dn1 = units_ffn(1)
            plan.append((i, gu, dn, qk, vv, wo_u, gu1, dn1))

        _, gu, dn, qk, vv = plan[0]
        pool_layer(NMETA, [(meta_d, NMETA)])
        ffn(NMETA, 0, gu, dn)
        qkv(NMETA, [NMETA], 0, 0, qk, vv, want_q=False)
        for i in range(nsup):
            _, gu, dn, qk, vv, wo_u, gu1, dn1 = plan[1 + i]
            rows = [(x_d[i * TS + tt * 128:i * TS + (tt + 1) * 128, :], 128) for tt in range(4)]
            pool_layer(TS, rows, pre=(i > 0))
            ffn(TS, 0, gu, dn)
            qkv(TS, [128] * 4, 1 + 4 * i, NMETA + TS * i, qk, vv, want_q=True)
            attention(i, wo_u)
            if i + 1 < nsup:
                nsrc = x_d[(i + 1) * TS:(i + 1) * TS + 128, :]
                S.dma("sp", lambda e, nsrc=nsrc: e.dma_start(out=xpre[:, :], in_=nsrc), writes=["xpre"])
            ffn(TS, 1, gu1, dn1)
            write_out(i)
        S.build(st)
    return nc


_CACHE = {}


def kernel(x, meta_tokens, pool_w, pool_scale, fox_w_in, fox_b_f, fox_w_o,
           ffn_w_gate, ffn_w_up, ffn_w_down, ln_g, ln_b):
    f = lambda a: np.ascontiguousarray(np.asarray(a, dtype=np.float32))
    x = f(x)
    B = x.shape[0]
    if "nc" not in _CACHE:
        _CACHE["nc"] = build_program()
    nc = _CACHE["nc"]
    vecs = np.concatenate([f(ln_g).reshape(32, 128), f(ln_b).reshape(32, 128), f(pool_scale).reshape(8, 128)], axis=0)
    consts = make_consts()
    shared = {
        "meta": f(meta_tokens), "pool_w": f(pool_w)[0], "w_in": f(fox_w_in)[0], "w_o": f(fox_w_o)[0],
        "w_gate": f(ffn_w_gate), "w_up": f(ffn_w_up), "w_down": f(ffn_w_down),
        "vecs": np.ascontiguousarray(vecs),
        "bf_bc": np.ascontiguousarray(np.broadcast_to(f(fox_b_f).reshape(1, NH), (128, NH))),
    }
    shared.update(consts)
    in_maps = [dict(shared, x=x[b]) for b in range(B)]
    res = run_bass_kernel_spmd(nc, in_maps, core_ids=list(range(B)))
    return np.stack([np.asarray(r["out"], dtype=np.float32) for r in res.results], axis=0)
```
